# Optimizing a Trainium2 kernel written in Bass

```python
import math
import jax, jax.numpy as jnp
from jax import lax
import numpy as np

D_MODEL = 2048
BATCH = 8
SEQ = 2048
DEPTH = 1
DEC_BATCH = 8
DEC_SEQ = 64
PAST_LEN = 1024

CHUNK = 64
D_INNER = 2 * D_MODEL
SSD_HEADDIM = 64
SSD_HEADS = D_INNER // SSD_HEADDIM
SSD_GROUPS = 8
D_STATE = 128
CONV_W = 4
CONV_DIM = D_INNER + 2 * SSD_GROUPS * D_STATE
SB_HEADDIM = 128
SB_HEADS = D_MODEL // SB_HEADDIM
D_ATTN = SB_HEADS * SB_HEADDIM
SB_QBLOCK = 128
SB_SCALE = 1.0 / math.sqrt(SB_HEADDIM)
D_FF = 4 * D_MODEL
ALPHA = (2.0 * DEPTH) ** 0.25
BETA = (8.0 * DEPTH) ** -0.25
LN_EPS = 1e-5
RMS_EPS = 1e-5
SPLITS = list(np.cumsum([D_INNER, CONV_DIM, SSD_HEADS, D_ATTN, D_ATTN, D_ATTN]))
D_IN_PROJ = D_INNER + CONV_DIM + SSD_HEADS + 3 * D_ATTN + 2 * D_MODEL

kernel_name = "ssd_stickbreak_gated_deepnorm_stream"


def layer_norm(x, g, b):
    xf = x.astype(jnp.float32)
    mu = jnp.mean(xf, axis=-1, keepdims=True)
    var = jnp.mean(jnp.square(xf - mu), axis=-1, keepdims=True)
    return ((xf - mu) * lax.rsqrt(var + LN_EPS) * g + b).astype(x.dtype)


def gated_rmsnorm(y, z, w):
    sh = y.shape
    yg = (y * jax.nn.silu(z)).astype(jnp.float32).reshape(sh[:-1] + (SSD_GROUPS, D_INNER // SSD_GROUPS))
    yg = yg * lax.rsqrt(jnp.mean(jnp.square(yg), axis=-1, keepdims=True) + RMS_EPS)
    return (yg.reshape(sh) * w).astype(y.dtype)


def causal_conv(xbc, prev, w, b):
    T = xbc.shape[1]
    xp = jnp.concatenate([prev.astype(xbc.dtype), xbc], axis=1)
    y = b + sum(xp[:, j:j + T] * w[j] for j in range(CONV_W))
    return jax.nn.silu(y), xp[:, -(CONV_W - 1):]


def segsum(a):
    T = a.shape[-1]
    ar = jnp.broadcast_to(a[..., :, None], a.shape + (T,))
    ar = jnp.where(jnp.tril(jnp.ones((T, T), bool), -1), ar, 0.0)
    cs = jnp.cumsum(ar, axis=-2)
    return jnp.where(jnp.tril(jnp.ones((T, T), bool)), cs, -jnp.inf)


def ssd_scan(xh, dt, a, bm, cm, h0):
    b, T, H, P = xh.shape
    G, N = bm.shape[2], bm.shape[3]
    R = H // G
    lc = min(CHUNK, T)
    nc = T // lc
    xd = (xh * dt[..., None]).reshape(b, nc, lc, G, R, P)
    dA = (dt * a).reshape(b, nc, lc, G, R).transpose(0, 3, 4, 1, 2)
    Bc = bm.reshape(b, nc, lc, G, N)
    Cc = cm.reshape(b, nc, lc, G, N)
    a_cs = jnp.cumsum(dA, axis=-1)
    Lm = jnp.exp(segsum(dA))
    y_diag = jnp.einsum('bclgn,bcsgn,bgrcls,bcsgrp->bclgrp', Cc, Bc, Lm, xd)
    decay_states = jnp.exp(a_cs[..., -1:] - a_cs)
    states = jnp.einsum('bclgn,bgrcl,bclgrp->bcgrpn', Bc, decay_states, xd)
    h0r = h0.reshape(b, G, R, P, N).astype(states.dtype)
    states = jnp.concatenate([h0r[:, None], states], axis=1)
    chunk_tot = jnp.pad(a_cs[..., -1], ((0, 0), (0, 0), (0, 0), (1, 0)))
    decay_chunk = jnp.exp(segsum(chunk_tot))
    new_states = jnp.einsum('bgrzc,bcgrpn->bzgrpn', decay_chunk, states)
    states_in, h_last = new_states[:, :-1], new_states[:, -1]
    y_off = jnp.einsum('bclgn,bcgrpn,bgrcl->bclgrp', Cc, states_in, jnp.exp(a_cs))
    y = (y_diag + y_off).reshape(b, T, H, P)
    return y.astype(xh.dtype), h_last.reshape(b, H, P, N).astype(h0.dtype)


def sb_block(qb, k, v, pos0):
    Q, Tk = qb.shape[1], k.shape[1]
    z = jnp.einsum('bqhd,bkhd->bhqk', qb, k).astype(jnp.float32) * SB_SCALE
    mask = jnp.arange(Tk)[None, :] < (pos0 + jnp.arange(Q))[:, None]
    log_1m = jnp.where(mask, jax.nn.log_sigmoid(-z), 0.0)
    suffix = lax.cumsum(log_1m, axis=3, reverse=True) - log_1m
    w = jnp.where(mask, jnp.exp(jax.nn.log_sigmoid(z) + suffix), 0.0)
    return jnp.einsum('bhqk,bkhd->bqhd', w.astype(v.dtype), v)


def sb_attention(q, k, v):
    b, T, H, Dh = q.shape
    off = k.shape[1] - T
    if T > SB_QBLOCK and T % SB_QBLOCK == 0:
        nb = T // SB_QBLOCK
        qb = q.reshape(b, nb, SB_QBLOCK, H, Dh).transpose(1, 0, 2, 3, 4)
        starts = off + jnp.arange(nb) * SB_QBLOCK
        out = lax.map(lambda a: sb_block(a[0], k, v, a[1]), (qb, starts))
        return out.transpose(1, 0, 2, 3, 4).reshape(b, T, H, Dh)
    return sb_block(q, k, v, off)


def trunk_layer(x, conv_prev, h0, k_past, v_past, w_in, b_gate, conv_w, conv_b, dt_bias, a_log,
                d_skip, ssm_norm_w, w_br_ssd, w_br_attn, w_out, ln1_g, ln1_b, w_up, b_up,
                w_down, b_down, ln2_g, ln2_b):
    b, T, _ = x.shape
    proj = jnp.einsum('btd,de->bte', x, w_in)
    z, xbc, dt_raw, q, k, v, gate_logits = jnp.split(proj, SPLITS, axis=-1)
    xbc, conv_new = causal_conv(xbc, conv_prev, conv_w, conv_b)
    xs, bm, cm = jnp.split(xbc, [D_INNER, D_INNER + SSD_GROUPS * D_STATE], axis=-1)
    xh = xs.reshape(b, T, SSD_HEADS, SSD_HEADDIM)
    bm = bm.reshape(b, T, SSD_GROUPS, D_STATE)
    cm = cm.reshape(b, T, SSD_GROUPS, D_STATE)
    dt = jax.nn.softplus(dt_raw.astype(jnp.float32) + dt_bias)
    a = -jnp.exp(a_log.astype(jnp.float32))
    y_ssd, h_new = ssd_scan(xh, dt, a, bm, cm, h0)
    y_ssd = (y_ssd + d_skip[:, None] * xh).reshape(b, T, D_INNER)
    y_ssd = gated_rmsnorm(y_ssd, z, ssm_norm_w)
    q = q.reshape(b, T, SB_HEADS, SB_HEADDIM)
    k = k.reshape(b, T, SB_HEADS, SB_HEADDIM)
    v = v.reshape(b, T, SB_HEADS, SB_HEADDIM)
    k_all = jnp.concatenate([k_past.astype(k.dtype), k], axis=1)
    v_all = jnp.concatenate([v_past.astype(v.dtype), v], axis=1)
    y_sb = sb_attention(q, k_all, v_all).reshape(b, T, D_ATTN)
    g_ssd, g_sb = jnp.split(jax.nn.sigmoid(gate_logits + b_gate), 2, axis=-1)
    merged = g_ssd * (y_ssd @ w_br_ssd) + g_sb * (y_sb @ w_br_attn)
    x1 = layer_norm(ALPHA * x + merged @ w_out, ln1_g, ln1_b)
    hdn = jnp.square(jax.nn.relu(x1 @ w_up + b_up))
    x2 = layer_norm(ALPHA * x1 + hdn @ w_down + b_down, ln2_g, ln2_b)
    return x2, conv_new, h_new, k, v


def setup_inputs(seed: int = 0) -> dict:
    key = jax.random.key(seed)
    ks = jax.random.split(key, 26)

    def nrm(k, shape, scale):
        return jax.random.normal(k, shape, jnp.float32) * scale

    dt0 = jnp.exp(jax.random.uniform(ks[9], (DEPTH, SSD_HEADS), jnp.float32, math.log(1e-3), math.log(1e-1)))
    return {
        "x_prompt": nrm(ks[0], (BATCH, SEQ, D_MODEL), 1.0),
        "x_sample": nrm(ks[1], (DEC_BATCH, DEC_SEQ, D_MODEL), 1.0),
        "cache_conv": nrm(ks[2], (DEPTH, DEC_BATCH, CONV_W - 1, CONV_DIM), 1.0),
        "state_ssm": nrm(ks[3], (DEPTH, DEC_BATCH, SSD_HEADS, SSD_HEADDIM, D_STATE), 0.1),
        "cache_k": nrm(ks[4], (DEPTH, DEC_BATCH, PAST_LEN, SB_HEADS, SB_HEADDIM), 1.0),
        "cache_v": nrm(ks[5], (DEPTH, DEC_BATCH, PAST_LEN, SB_HEADS, SB_HEADDIM), 1.0),
        "w_in": nrm(ks[6], (DEPTH, D_MODEL, D_IN_PROJ), D_MODEL ** -0.5),
        "b_gate": nrm(ks[7], (DEPTH, 2 * D_MODEL), 0.02),
        "conv_w": nrm(ks[8], (DEPTH, CONV_W, CONV_DIM), CONV_W ** -0.5),
        "conv_b": nrm(ks[10], (DEPTH, CONV_DIM), 0.02),
        "dt_bias": dt0 + jnp.log(-jnp.expm1(-dt0)),
        "a_log": jnp.log(jax.random.uniform(ks[11], (DEPTH, SSD_HEADS), jnp.float32, 1.0, 16.0)),
        "d_skip": 1.0 + nrm(ks[12], (DEPTH, SSD_HEADS), 0.1),
        "ssm_norm_w": 1.0 + nrm(ks[13], (DEPTH, D_INNER), 0.02),
        "w_br_ssd": nrm(ks[14], (DEPTH, D_INNER, D_MODEL), D_INNER ** -0.5),
        "w_br_attn": nrm(ks[15], (DEPTH, D_ATTN, D_MODEL), D_ATTN ** -0.5),
        "w_out": nrm(ks[16], (DEPTH, D_MODEL, D_MODEL), BETA * D_MODEL ** -0.5),
        "ln1_g": 1.0 + nrm(ks[17], (DEPTH, D_MODEL), 0.02),
        "ln1_b": nrm(ks[18], (DEPTH, D_MODEL), 0.02),
        "w_up": nrm(ks[19], (DEPTH, D_MODEL, D_FF), D_MODEL ** -0.5),
        "b_up": nrm(ks[20], (DEPTH, D_FF), 0.02),
        "w_down": nrm(ks[21], (DEPTH, D_FF, D_MODEL), BETA * D_FF ** -0.5),
        "b_down": nrm(ks[22], (DEPTH, D_MODEL), 0.02),
        "ln2_g": 1.0 + nrm(ks[23], (DEPTH, D_MODEL), 0.02),
        "ln2_b": nrm(ks[24], (DEPTH, D_MODEL), 0.02),
    }


def reference(x_prompt, x_sample, cache_conv, state_ssm, cache_k, cache_v, w_in, b_gate, conv_w,
              conv_b, dt_bias, a_log, d_skip, ssm_norm_w, w_br_ssd, w_br_attn, w_out, ln1_g, ln1_b,
              w_up, b_up, w_down, b_down, ln2_g, ln2_b):
    hp, hs = x_prompt, x_sample
    conv_p, ssm_p, k_p, v_p = [], [], [], []
    conv_s, ssm_s, k_s, v_s = [], [], [], []
    for l in range(DEPTH):
        wl = (w_in[l], b_gate[l], conv_w[l], conv_b[l], dt_bias[l], a_log[l], d_skip[l],
              ssm_norm_w[l], w_br_ssd[l], w_br_attn[l], w_out[l], ln1_g[l], ln1_b[l],
              w_up[l], b_up[l], w_down[l], b_down[l], ln2_g[l], ln2_b[l])
        zc = jnp.zeros((BATCH, CONV_W - 1, CONV_DIM), hp.dtype)
        zh = jnp.zeros((BATCH, SSD_HEADS, SSD_HEADDIM, D_STATE), hp.dtype)
        zk = jnp.zeros((BATCH, 0, SB_HEADS, SB_HEADDIM), hp.dtype)
        hp, c1, s1, k1, v1 = trunk_layer(hp, zc, zh, zk, zk, *wl)
        conv_p.append(c1); ssm_p.append(s1); k_p.append(k1); v_p.append(v1)
        hs, c2, s2, k2, v2 = trunk_layer(hs, cache_conv[l], state_ssm[l], cache_k[l], cache_v[l], *wl)
        conv_s.append(c2); ssm_s.append(s2); k_s.append(k2); v_s.append(v2)
    return (hp, hs, jnp.stack(conv_p), jnp.stack(ssm_p), jnp.stack(k_p), jnp.stack(v_p),
            jnp.stack(conv_s), jnp.stack(ssm_s), jnp.stack(k_s), jnp.stack(v_s))
```

```python
import numpy as np
import concourse.bass as bass
import concourse.mybir as mybir
from concourse.bass_utils import run_bass_kernel_spmd

F32 = mybir.dt.float32
BF16 = mybir.dt.bfloat16
ALU = mybir.AluOpType
AF = mybir.ActivationFunctionType

D = 2048
DI = 4096
NH = 64
HP = 64
NG = 8
NS = 128
CD = 6144
AH = 16
AD = 128
DFF = 8192
DIP = 20544
LP = 1024
TS = 64
ALPHA = 2.0 ** 0.25
LN_EPS = 1e-5
RMS_EPS = 1e-5
SB_SCALE = 1.0 / (128.0 ** 0.5)

PV_BG = 0
PV_CW = 32
PV_CB = PV_CW + 192
PV_NW = PV_CB + 48
PV_L1G = PV_NW + 32
PV_L1B = PV_L1G + 16
PV_L2G = PV_L1B + 16
PV_L2B = PV_L2G + 16
PV_BD = PV_L2B + 16
PV_BU = PV_BD + 16
PV_DS = PV_BU + 64
PV_DTB = PV_DS + 32
PV_ALOG = PV_DTB + 1
NPV = PV_ALOG + 1
XT_ENG = None
PRECAST = True
STQ = "scalar"
WIN_PLAN = [(0, 4096), (4096, 10240), (10240, 10304), (10304, 12352), (12352, 14400), (14400, 16448), (16448, 20544)]
WIN_ITEMS = sum((c1 - c0 + 255) // 256 for c0, c1 in WIN_PLAN)
CAST_ENG = None
EVAC_ENG = None


class _Op:
    __slots__ = ("eng", "fn", "deps", "dma", "flag", "sem", "val")


class Sched:
    ENGS = ("sync", "tensor", "vector", "scalar", "gpsimd")
    KDMA = 8

    def __init__(self):
        self.ops = []
        self.lastw = {}
        self.readers = {}
        self.fence = {}

    def add(self, eng, fn, r=(), w=(), dma=False):
        i = len(self.ops)
        deps = set()
        for k in r:
            j = self.lastw.get(k)
            if j is not None:
                deps.add(j)
        for k in w:
            j = self.lastw.get(k)
            if j is not None:
                deps.add(j)
            rs = self.readers.get(k)
            if rs:
                deps.update(rs)
        if eng in self.fence:
            deps |= self.fence.pop(eng)
        op = _Op()
        op.eng = eng
        op.fn = fn
        op.dma = dma
        op.flag = dma
        if eng == "tensor" and not dma:
            deps = {j for j in deps if not (self.ops[j].eng == "tensor" and not self.ops[j].dma)}
        op.deps = deps
        self.ops.append(op)
        for k in w:
            self.lastw[k] = i
            self.readers[k] = []
        for k in r:
            if k in self.lastw:
                self.readers[k].append(i)
            else:
                self.readers.setdefault(k, [])
        return i

    def barrier(self):
        last = {}
        dmas = set()
        for i, op in enumerate(self.ops):
            if op.dma:
                dmas.add(i)
            else:
                last[op.eng] = i
        per_q = {}
        for i in sorted(dmas):
            per_q.setdefault(self.ops[i].eng, []).append(i)
        f = set(last.values())
        for q, lst in per_q.items():
            f.update(lst[-self.KDMA:])
        for e in self.ENGS:
            self.fence[e] = set(f) | self.fence.get(e, set())

    def emit(self, nc):
        ops = self.ops
        for op in ops:
            for j in op.deps:
                ops[j].flag = True
        self.barrier()
        fin = self.add("sync", None)
        for j in ops[fin].deps:
            ops[j].flag = True
        import contextlib
        with contextlib.ExitStack() as st:
            csem = {e: st.enter_context(nc.semaphore("c_" + e)) for e in self.ENGS}
            dsem = {e: [st.enter_context(nc.semaphore("d_%s%d" % (e, k))) for k in range(self.KDMA)]
                    for e in ("sync", "gpsimd", "scalar")}
            ccount = {e: 0 for e in self.ENGS}
            dcount = {e: 0 for e in dsem}
            prev_dma = {}
            for i, op in enumerate(ops):
                if op.dma:
                    n = dcount[op.eng]
                    dcount[op.eng] += 1
                    op.sem = dsem[op.eng][n % self.KDMA]
                    op.val = 16 * (n // self.KDMA + 1)
                    if n >= self.KDMA:
                        op.deps.add(prev_dma[(op.eng, n - self.KDMA)])
                    prev_dma[(op.eng, n)] = i
                elif op.flag:
                    ccount[op.eng] += 1
                    op.sem = csem[op.eng]
                    op.val = ccount[op.eng]
            per_eng = {e: [i for i, op in enumerate(ops) if op.eng == e] for e in self.ENGS}
            block = st.enter_context(nc.Block())

            def run(e, name):
                waited = {}
                for i in per_eng[name]:
                    op = ops[i]
                    need = {}
                    for j in op.deps:
                        pj = ops[j]
                        key = id(pj.sem)
                        if key not in need or need[key][1] < pj.val:
                            need[key] = (pj.sem, pj.val)
                    for key, (sem, val) in need.items():
                        if waited.get(key, 0) >= val:
                            continue
                        waited[key] = val
                        e.wait_ge(sem, val)
                    if op.fn is None:
                        continue
                    ins = op.fn(e)
                    if op.dma:
                        ins.then_inc(op.sem, 16)
                    elif op.flag:
                        ins.then_inc(op.sem, 1)

            @block.sync
            def _(e):
                run(e, "sync")

            @block.tensor
            def _(e):
                run(e, "tensor")

            @block.vector
            def _(e):
                run(e, "vector")

            @block.scalar
            def _(e):
                run(e, "scalar")

            @block.gpsimd
            def _(e):
                run(e, "gpsimd")


class Arena:
    def __init__(self, nc, base=16512):
        self.nc = nc
        self.off = base
        self.n = 0

    def mark(self):
        return self.off

    def reset(self, off):
        self.off = off

    def alloc(self, shape, dt, at=None):
        nbytes = int(np.prod(shape[1:])) * (4 if dt == F32 else 2)
        nbytes = (nbytes + 63) // 64 * 64
        if at is None:
            at = self.off
            self.off += nbytes
        assert at + nbytes <= 229376, ("SBUF overflow", at, nbytes)
        self.n += 1
        return self.nc.alloc_sbuf_tensor_at("a%d" % self.n, list(shape), dt, offset=at)


def build(TP, dbg=False, stop=None):
    TT = TP + TS
    NTB = [(t, min(512, TP - t)) for t in range(0, TP, 512)] + [(TP, TS)]
    SEQS = [(0, TP, 0), (TP, TS, LP)]
    nc = bass.Bass("TRN2", target_bir_lowering=False)
    S = Sched()

    def din(name, shape, dt=F32):
        return nc.dram_tensor(name, list(shape), dt, kind="ExternalInput").ap()

    def dout(name, shape, dt=F32):
        return nc.dram_tensor(name, list(shape), dt, kind="ExternalOutput").ap()

    def dscr(name, shape, dt=F32):
        return nc.dram_tensor(name, list(shape), dt, kind="ExternalOutput" if dbg else "Internal").ap()

    xT = din("xT", [D, TT])
    convp = din("convp", [128, 48, 3])
    s0T = din("s0T", [128, NH * HP])
    ckT = din("ckT", [AH, 128, LP])
    cv = din("cv", [LP, D])
    w_in = din("w_in", [WIN_ITEMS, 128, 4096])
    w_br_ssd = din("w_br_ssd", [16, 128, 4096])
    w_br_attn = din("w_br_attn", [8, 128, 4096])
    w_out = din("w_out", [8, 128, 4096])
    w_up = din("w_up", [32, 128, 4096])
    w_down = din("w_down", [32, 128, 4096])
    wb_scr = {"ssd": dscr("wb_ssd", [16, 128, 4096], BF16), "attn": dscr("wb_attn", [8, 128, 4096], BF16),
              "out": dscr("wb_out", [8, 128, 4096], BF16), "up": dscr("wb_up", [32, 128, 4096], BF16),
              "down": dscr("wb_down", [32, 128, 4096], BF16)}
    pvec_d = din("pvec", [128, NPV])
    consts_d = din("consts", [128, 5 * 128])

    y_o = dout("y_o", [TT, D])
    conv_o = [dout("conv_p", [3, CD]), dout("conv_s", [3, CD])]
    ssm_o = [dout("ssm_p", [NH * HP, NS]), dout("ssm_s", [NH * HP, NS])]
    k_o = dout("k_o", [TT, D])
    v_o = dout("v_o", [TT, D])

    zT = dscr("zT", [DI, TT])
    xbcT = dscr("xbcT", [CD, TT])
    xbcTb = dscr("xbcTb", [CD, TT], BF16)
    dtT = dscr("dtT", [NH, TT])
    qT = dscr("qT", [D, TT], BF16)
    kT = dscr("kT", [D, TT], BF16)
    vtm = dscr("vtm", [TT, D], BF16)
    gT = dscr("gT", [DI, TT])
    ynT = dscr("ynT", [DI, TT], BF16)
    ysbT = dscr("ysbT", [D, TT], BF16)

    A = Arena(nc)
    PC_PER = max(1, -(-384 // (8 * max(1, TP // 128))))
    consts = A.alloc([128, 5 * 128], F32)
    pvec = A.alloc([128, NPV], F32)
    identb = A.alloc([128, 128], BF16)
    ident = consts[:, 0:128]
    ones = consts[:, 128:256]
    Umat = consts[:, 256:384]
    LE = consts[:, 384:512]
    ones512 = consts[:, 512:640]
    S.add("sync", lambda e: e.dma_start(out=consts[:], in_=consts_d), w=["consts"], dma=True)
    S.add("sync", lambda e: e.dma_start(out=pvec[:], in_=pvec_d), w=["pvec"], dma=True)
    S.add("vector", lambda e: e.tensor_copy(out=identb[:], in_=ident), r=["consts"], w=["identb"])
    constsb = A.alloc([128, 3 * 128], BF16)
    S.add("vector", lambda e: e.tensor_copy(out=constsb[:], in_=consts[:, 128:512]), r=["consts"], w=["constsb"])
    onesb = constsb[:, 0:128]
    Ub = constsb[:, 128:256]
    LEb = constsb[:, 256:384]
    S.add("scalar", lambda e: e.activation(out=pvec[64:128, PV_ALOG:PV_ALOG + 1], in_=pvec[64:128, PV_ALOG:PV_ALOG + 1],
                                           func=AF.Exp), r=["pvec"], w=["pvec"])
    S.add("vector", lambda e: e.tensor_scalar(out=pvec[64:128, PV_ALOG:PV_ALOG + 1], in0=pvec[64:128, PV_ALOG:PV_ALOG + 1],
                                              scalar1=-1.0, scalar2=None, op0=ALU.mult), r=["pvec"], w=["pvec"])

    def pv(col, n=1):
        return pvec[:, col:col + n]

    PS = [nc.alloc_psum_tensor("ps%d" % i, [128, 512], F32) for i in range(8)]
    state = {"evac": 0, "cast": 0, "gbank": 0, "mbank": 0}

    def evac_eng():
        state["evac"] ^= 1
        if state.get("force_evac"):
            return state["force_evac"]
        if EVAC_ENG is not None:
            return EVAC_ENG
        return "scalar" if state["evac"] else "vector"

    def copy_op(eng, out, in_):
        if eng == "scalar":
            return lambda e: e.activation(out=out, in_=in_, func=AF.Copy)
        return lambda e: e.tensor_copy(out=out, in_=in_)

    def mbank():
        state["mbank"] = (state["mbank"] + 1) % state.get("mb_n", 4)
        return state.get("mb_lo", 4) + state["mbank"]

    base_mark = A.mark()

    def gemm(Wt, item_base, K, col_sbs, tbs, act, act_keys, evac_cb, wst, wbf, mode="cast", Wbs=None):
        nseg = K // 2048
        assert nseg == 1 or len(tbs) == 1
        items = [(sb, seg) for sb in range(len(col_sbs)) for seg in range(nseg)]
        if mode == "bf16":
            bufs = list(wbf)
            for b in range(2):
                v = wst[b][:].rearrange("p a b -> p (a b)").bitcast(BF16)
                bufs.append(("alias", b, v[:, 0:4096].rearrange("p (a b) -> p a b", b=256)))
                bufs.append(("alias", b, v[:, 4096:8192].rearrange("p (a b) -> p a b", b=256)))
        else:
            bufs = list(wbf)
        NB = len(bufs)

        def buf_ap(j):
            bb = bufs[j]
            return bb[2] if isinstance(bb, tuple) else bb[:]

        def buf_keys(j):
            bb = bufs[j]
            return [("wbf", j)] + ([("wst", bb[1])] if isinstance(bb, tuple) else [])

        def load(ii):
            it = item_base + ii
            j = ii % NB
            if mode == "bf16":
                if isinstance(bufs[j], tuple):
                    S.add("sync", lambda e: e.dma_start(out=buf_ap(j), in_=Wbs[it].rearrange("p (a b) -> p a b", b=256)),
                          r=[("wbscr", id(Wbs), it)] + [("wbscr", id(Wbs), it, q) for q in range(4)], w=buf_keys(j), dma=True)
                else:
                    S.add("sync", lambda e: e.dma_start(out=bufs[j][:].rearrange("p a b -> p (a b)"), in_=Wbs[it]),
                          r=[("wbscr", id(Wbs), it)] + [("wbscr", id(Wbs), it, q) for q in range(4)], w=buf_keys(j), dma=True)
                return
            b = j
            S.add("sync", lambda e: e.dma_start(out=wst[b][:].rearrange("p a b -> p (a b)"), in_=Wt[it]),
                  w=[("wst", b)], dma=True)
            state["cast"] ^= 1
            ce = "gpsimd" if state["cast"] else "vector"
            if CAST_ENG is not None:
                ce = CAST_ENG
            if state.get("force_cast"):
                ce = state["force_cast"]
            S.add(ce, lambda e: e.tensor_copy(out=wbf[b][:], in_=wst[b][:]), r=[("wst", b)], w=[("wbf", b)])

        def store(ii):
            it = item_base + ii
            b = ii % NB
            S.add("sync", lambda e: e.dma_start(out=Wbs[it], in_=wbf[b][:].rearrange("p a b -> p (a b)")),
                  r=[("wbf", b)], w=[("wbscr", id(Wbs), it)], dma=True)

        for ii in range(min(NB - 1, len(items))):
            load(ii)
        banks = {}
        for ii, (sb, seg) in enumerate(items):
            if ii + NB - 1 < len(items):
                load(ii + NB - 1)
            col0, ncols = col_sbs[sb]
            j = ii % NB
            wap = buf_ap(j)
            for s0 in range(0, ncols, 128):
                n = min(128, ncols - s0)
                for (tbi, nt) in tbs:
                    if seg == 0:
                        state["gbank"] = (state["gbank"] + 1) % 4
                        banks[(sb, tbi, s0)] = state["gbank"]
                    bk = banks[(sb, tbi, s0)]
                    for kc in range(16):
                        a_ap = act(seg * 16 + kc, tbi)
                        st_ = (seg == 0 and kc == 0)
                        sp_ = (seg == nseg - 1 and kc == 15)
                        S.add("tensor",
                              lambda e, bk=bk, n=n, nt=nt, kc=kc, s0=s0, a_ap=a_ap, st_=st_, sp_=sp_, wap=wap:
                              e.matmul(PS[bk][0:n, 0:nt], lhsT=wap[:, kc, s0:s0 + n], rhs=a_ap, start=st_, stop=sp_),
                              r=[("wbf", j)] + act_keys(seg * 16 + kc, tbi), w=[("ps", bk)])
                    if seg == nseg - 1:
                        evac_cb(col0 + s0, n, tbi, nt, PS[bk][0:n, 0:nt], ("ps", bk))
            if mode == "cast_store":
                store(ii)

    def sbs_range(c0, c1):
        return [(c, min(256, c1 - c)) for c in range(c0, c1, 256)]

    XT = A.alloc([128, 16, TT], BF16)
    wst = [A.alloc([128, 16, 256], F32) for _ in range(2)]
    wbf = [A.alloc([128, 16, 256], BF16) for _ in range(2)]
    XST_AT = A.mark()
    xst = [A.alloc([128, 512], F32) for _ in range(2)]
    stA = [A.alloc([128, TT], F32) for _ in range(2)]
    stX = [A.alloc([128, TT + 6], F32) for _ in range(2)]
    cvo = [A.alloc([128, TT], F32) for _ in range(2)]
    cvb = [A.alloc([128, TT], BF16) for _ in range(2)]
    cph = A.alloc([128, 48, 3], F32)
    cst = nc.alloc_sbuf_tensor_at("cst_alias", [128, 48, 2, 3], F32, offset=XST_AT)
    S.add("sync", lambda e: e.dma_start(out=cph[:], in_=convp), w=["cph"], dma=True)
    for b_ in range(2):
        S.add("gpsimd", lambda e, b_=b_: e.memset(stX[b_][:, 0:3], 0.0), w=[("stXz", b_)])
    stB = [A.alloc([128, TT], BF16) for _ in range(2)]
    tks = A.alloc([128, 17, 128], F32)
    tkb = A.alloc([128, 17, 128], BF16)
    xTv = xT.rearrange("(c p) t -> p c t", p=128)
    stg = [stX[0][:, 3:3 + TT], stX[1][:, 3:3 + TT], cvo[0][:, 0:TT], cvo[1][:, 0:TT]]
    for kc in range(16):
        i_ = kc % 4
        S.add("sync", lambda e, kc=kc, i_=i_: e.dma_start(out=stg[i_], in_=xTv[:, kc, :]), w=[("stg", i_)], dma=True)
        eng = "vector" if kc % 2 == 0 else "scalar"
        S.add(eng, copy_op(eng, XT[:, kc, :], stg[i_]), r=[("stg", i_)], w=[("XT", kc, t0_) for (t0_, _n) in NTB])
    S.barrier()

    if stop == "A0":
        S.emit(nc)
        return nc
    stctr = {"n": 0}
    NT_P = TP // 128

    def tok_tiles():
        return [(i * 128, 128) for i in range(NT_P)] + [(TP, TS)]

    deferred = []

    def make_in_evac(kind, dst, row0, c_base):
        cur = {}

        def cb(col, n, tbi, nt, ps, pskey):
            t0 = NTB[tbi][0]
            if tbi == 0:
                stctr["n"] += 1
                cur["b"] = stctr["n"] % 2
            b = cur["b"]
            r0 = col - c_base + row0
            if kind == "xbc":
                eng = evac_eng()
                xo = t0 + 3 + (3 if t0 >= TP else 0)
                S.add(eng, copy_op(eng, stX[b][0:n, xo:xo + nt], ps), r=[pskey], w=[("stX", b, tbi)])
                if tbi == len(NTB) - 1:
                    while deferred:
                        deferred.pop(0)()
                    c = r0 // 128
                    sxk = [("stX", b, i) for i in range(len(NTB))] + [("stXz", b)]
                    S.add("scalar", lambda e: e.activation(out=stX[b][:, TP + 3:TP + 6], in_=cph[:, c, :], func=AF.Copy),
                          r=["cph"], w=[("stXh", b)])
                    sxk = sxk + [("stXh", b)]
                    S.add("scalar", lambda e: e.activation(out=cst[:, c, 0, :], in_=stX[b][:, TP:TP + 3], func=AF.Copy),
                          r=sxk, w=[("cst", c, 0), ("xst", 0)])
                    S.add("scalar", lambda e: e.activation(out=cst[:, c, 1, :], in_=stX[b][:, TT + 3:TT + 6], func=AF.Copy),
                          r=sxk, w=[("cst", c, 1)])
                    NQ = 4 if TP >= 1024 else 1
                    segs = [(q * (TP // NQ), q * (TP // NQ), TP // NQ) for q in range(NQ)] + [(TP, TP + 3, TS)]
                    for (o0, i0, L) in segs:
                        S.add("scalar", lambda e, o0=o0, i0=i0, L=L: e.activation(
                            out=cvo[b][:, o0:o0 + L], in_=stX[b][:, i0:i0 + L], func=AF.Identity,
                            scale=pvec[:, PV_CW + c:PV_CW + c + 1], bias=pvec[:, PV_CB + c:PV_CB + c + 1]),
                            r=sxk + ["pvec"], w=[("cvo", b, o0)])
                    for j in range(1, 4):
                        for (o0, i0, L) in segs:
                            S.add("vector", lambda e, o0=o0, i0=i0, L=L, j=j: e.scalar_tensor_tensor(
                                out=cvo[b][:, o0:o0 + L], in0=stX[b][:, i0 + j:i0 + j + L],
                                scalar=pvec[:, PV_CW + 48 * j + c:PV_CW + 48 * j + c + 1], in1=cvo[b][:, o0:o0 + L],
                                op0=ALU.mult, op1=ALU.add), r=sxk + ["pvec", ("cvo", b, o0)], w=[("cvo", b, o0)])
                    cvk = [("cvo", b, sg[0]) for sg in segs]

                    def tail(b=b, c=c, r0=r0, n=n, cvk=cvk):
                        S.add("scalar", lambda e: e.activation(out=cvo[b][:], in_=cvo[b][:], func=AF.Silu), r=cvk, w=cvk)
                        S.add("scalar", lambda e: e.activation(out=cvb[b][:], in_=cvo[b][:], func=AF.Copy), r=cvk,
                              w=[("cvb", b)])
                        if c < 32:
                            S.add(STQ, lambda e: e.dma_start(out=dst[r0:r0 + n, :], in_=cvo[b][:]), r=cvk,
                                  w=[("scr", kind, r0)], dma=True)
                        S.add(STQ, lambda e: e.dma_start(out=xbcTb[r0:r0 + n, :], in_=cvb[b][:]), r=[("cvb", b)],
                              w=[("scr", "xbcb", r0)], dma=True)
                    deferred.append(tail)
            elif kind in ("z", "dt"):
                eng = evac_eng()
                S.add(eng, copy_op(eng, stA[b][0:n, t0:t0 + nt], ps), r=[pskey], w=[("stA", b, tbi)])
            elif kind == "g":
                S.add("scalar", lambda e: e.activation(out=stA[b][0:n, t0:t0 + nt], in_=ps, func=AF.Sigmoid,
                                                       bias=pvec[0:n, PV_BG + r0 // 128:PV_BG + r0 // 128 + 1]),
                      r=[pskey, "pvec"], w=[("stA", b, tbi)])
            elif kind == "q":
                eng = evac_eng()
                S.add(eng, copy_op(eng, stB[b][0:n, t0:t0 + nt], ps), r=[pskey], w=[("stB", b, tbi)])
            elif kind == "k":
                S.add("scalar", copy_op("scalar", stA[b][0:n, t0:t0 + nt], ps), r=[pskey], w=[("stA", b, tbi)])
                S.add("gpsimd", copy_op("gpsimd", stB[b][0:n, t0:t0 + nt], stA[b][0:n, t0:t0 + nt]), r=[("stA", b, tbi)],
                      w=[("stB", b, tbi)])
            elif kind == "v":
                S.add("scalar", copy_op("scalar", stA[b][0:n, t0:t0 + nt], ps), r=[pskey], w=[("stA", b, tbi)])
            if tbi == len(NTB) - 1:
                allk = [("stA", b, i) for i in range(len(NTB))]
                allkb = [("stB", b, i) for i in range(len(NTB))]
                if kind in ("z", "dt", "g"):
                    S.add(STQ, lambda e: e.dma_start(out=dst[r0:r0 + n, :], in_=stA[b][0:n, :]), r=allk,
                          w=[("scr", kind, r0)], dma=True)
                if kind in ("q", "k"):
                    S.add(STQ, lambda e: e.dma_start(out=dst[r0:r0 + n, :], in_=stB[b][0:n, :]), r=allkb,
                          w=[("scr", kind, r0)], dma=True)
                if kind in ("k", "v"):
                    h = r0 // 128
                    tl = tok_tiles()
                    for g0 in range(0, len(tl), 4):
                        bk = mbank()
                        grp = tl[g0:g0 + 4]
                        for j, (tt0, tn) in enumerate(grp):
                            S.add("tensor", lambda e, bk=bk, j=j, tt0=tt0, tn=tn:
                                  e.matmul(PS[bk][0:tn, j * 128:(j + 1) * 128], lhsT=stA[b][:, tt0:tt0 + tn], rhs=ident, start=True, stop=True),
                                  r=allk + ["consts"], w=[("ps", bk)])
                        full = [x for x in grp if x[1] == 128]
                        eng = evac_eng()
                        if full:
                            nf = len(full)
                            S.add(eng, copy_op(eng, tks[:, g0:g0 + nf, :],
                                               PS[bk][:, 0:nf * 128].rearrange("p (a b) -> p a b", b=128)),
                                  r=[("ps", bk)], w=[("tks", g0)])
                        if len(full) < len(grp):
                            j = len(full)
                            S.add(eng, copy_op(eng, tks[0:TS, g0 + j, :], PS[bk][0:TS, j * 128:(j + 1) * 128]),
                                  r=[("ps", bk)], w=[("tks", g0, "s")])
                    tkk = [("tks", g0) for g0 in range(0, len(tl), 4)] + [("tks", (len(tl) - 1) // 4 * 4, "s")]
                    dsto = k_o if kind == "k" else v_o
                    S.add(STQ, lambda e: e.dma_start(
                        out=dsto[0:TP, h * 128:(h + 1) * 128].rearrange("(n p) f -> p n f", p=128),
                        in_=tks[:, 0:NT_P, :]), r=tkk, w=[("out", kind, h, 0)], dma=True)
                    S.add(STQ, lambda e: e.dma_start(out=dsto[TP:TT, h * 128:(h + 1) * 128], in_=tks[0:TS, NT_P, :]),
                          r=tkk, w=[("out", kind, h, 1)], dma=True)
                    if kind == "v":
                        S.add("gpsimd", lambda e: e.tensor_copy(out=tkb[:, 0:NT_P, :], in_=tks[:, 0:NT_P, :]), r=tkk,
                              w=["tkb"])
                        S.add("gpsimd", lambda e: e.tensor_copy(out=tkb[0:TS, NT_P, :], in_=tks[0:TS, NT_P, :]), r=tkk,
                              w=["tkb2"])
                        S.add(STQ, lambda e: e.dma_start(
                            out=vtm[0:TP, h * 128:(h + 1) * 128].rearrange("(n p) f -> p n f", p=128),
                            in_=tkb[:, 0:NT_P, :]), r=["tkb"], w=[("scr", "vtm", h, 0)], dma=True)
                        S.add(STQ, lambda e: e.dma_start(out=vtm[TP:TT, h * 128:(h + 1) * 128], in_=tkb[0:TS, NT_P, :]),
                              r=["tkb2"], w=[("scr", "vtm", h, 1)], dma=True)
        return cb

    tbs_all = [(i, nt) for i, (t0, nt) in enumerate(NTB)]

    def actA(kc, tbi):
        t0, nt = NTB[tbi]
        return XT[:, kc, t0:t0 + nt]

    def actA_keys(kc, tbi):
        return [("XT", kc, NTB[tbi][0])]

    plan = [("z", 0, 4096, zT), ("xbc", 4096, 10240, xbcT), ("dt", 10240, 10304, dtT), ("q", 10304, 12352, qT),
            ("k", 12352, 14400, kT), ("v", 14400, 16448, None), ("g", 16448, 20544, gT)]
    ibase = 0
    for kind, c0, c1, dst in plan:
        if stop is not None and stop.startswith("A1") and kind == stop[2:]:
            S.emit(nc)
            return nc
        if kind == "xbc":
            state["force_cast"] = "gpsimd"
            state["force_evac"] = "scalar"
        gemm(w_in, ibase, D, sbs_range(c0, c1), tbs_all, actA, actA_keys, make_in_evac(kind, dst, 0, c0), wst, wbf)
        state["force_cast"] = None
        state["force_evac"] = None
        while deferred:
            deferred.pop(0)()
        if kind == "xbc":
            for sq_ in range(2):
                for c4 in range(12):
                    bk = mbank()
                    for j in range(4):
                        c_ = c4 * 4 + j
                        S.add("tensor", lambda e, bk=bk, j=j, c_=c_, sq_=sq_: e.matmul(
                            PS[bk][0:3, j * 128:(j + 1) * 128], lhsT=cst[:, c_, sq_, :], rhs=ident, start=True, stop=True),
                            r=[("cst", c_, sq_), "consts"], w=[("ps", bk)])
                    S.add("vector", copy_op("vector", xst[1][0:3, :], PS[bk][0:3, :]), r=[("ps", bk)], w=[("xst", 1)])
                    S.add(STQ, lambda e, sq_=sq_, c4=c4: e.dma_start(out=conv_o[sq_][:, c4 * 512:(c4 + 1) * 512],
                                                                     in_=xst[1][0:3, :]),
                          r=[("xst", 1)], w=[("out", "conv", sq_, c4)], dma=True)
        ibase += len(sbs_range(c0, c1))

    if stop == "A":
        S.emit(nc)
        return nc
    S.barrier()
    A.reset(base_mark)

    co = A.alloc([128, 32, 128], F32)
    sb16 = A.alloc([128, 48, 128], BF16)
    xs_tm = A.alloc([128, 64, 64], BF16)
    B_tm = A.alloc([128, 8, 128], BF16)
    ddtm = A.alloc([128, 128], F32)
    rhsDs = [(A.alloc([128, 8, 128], BF16), A.alloc([128, 8, 128], BF16)) for _ in range(4)]
    expDs = [A.alloc([128, 8, 128], BF16) for _ in range(4)]
    ecsbs = [A.alloc([128, 8, 128], BF16) for _ in range(4)]
    state["mb_lo"], state["mb_n"] = 0, 8
    GM_AT = A.mark()
    Gm = A.alloc([128, 8, 128], BF16)
    dd = A.alloc([128, 128], F32, at=GM_AT)
    xd = A.alloc([128, 64, 64], BF16)
    xdd = A.alloc([128, 64, 64], BF16)
    yTs = [A.alloc([128, 32, 128], F32) for _ in range(2)]
    zbufs = [A.alloc([128, 32, 128], F32) for _ in range(2)]
    yT = yTs[0]
    Sst = A.alloc([128, 64, 64], F32)
    Sbf = A.alloc([128, 64, 64], BF16)
    ynbs = [zb[:].rearrange("p a b -> p (a b)").bitcast(BF16)[:, 0:4096].rearrange("p (a b) -> p a b", b=128) for zb in zbufs]
    small = A.alloc([128, 6, 64], F32)
    dAhl = small[:, 5, :].bitcast(BF16).rearrange("p (a b) -> p a b", b=64)
    rstds = [A.alloc([128, 8, 128], F32) for _ in range(2)]
    xbcTv = xbcT.rearrange("(c p) t -> p c t", p=128)
    xbcTbv = xbcTb.rearrange("(c p) t -> p c t", p=128)
    zTv = zT.rearrange("(c p) t -> p c t", p=128)
    ynTv = ynT.rearrange("(c p) t -> p c t", p=128)
    alt = {"n": 0}

    def vp():
        alt["n"] ^= 1
        return "vector" if alt["n"] else "gpsimd"

    pcf = [A.alloc([128, 1024], F32) for _ in range(3)]
    pcb = [A.alloc([128, 1024], BF16) for _ in range(2)]

    def precast_gen():
        jobs = []
        for nm, Wt_ in (("ssd", w_br_ssd), ("attn", w_br_attn), ("out", w_out), ("up", w_up), ("down", w_down)):
            for it in range(Wt_.shape[0]):
                for q in range(4):
                    jobs.append((nm, Wt_, it, q))

        def ld(k):
            nm, Wt_, it, q = jobs[k]
            b = k % 3
            S.add("scalar", lambda e: e.dma_start(out=pcf[b][:], in_=Wt_[it][:, q * 1024:(q + 1) * 1024]),
                  w=[("pcf", b)], dma=True)

        ld(0)
        ld(1)
        for k, (nm, Wt_, it, q) in enumerate(jobs):
            if k + 2 < len(jobs):
                ld(k + 2)
            b = k % 2
            fb = k % 3
            S.add("scalar", lambda e, b=b, fb=fb: e.activation(out=pcb[b][:], in_=pcf[fb][:], func=AF.Copy),
                  r=[("pcf", fb)], w=[("pcb", b)])
            Wbs = wb_scr[nm]
            S.add("scalar", lambda e, b=b, Wbs=Wbs, it=it, q=q: e.dma_start(out=Wbs[it][:, q * 1024:(q + 1) * 1024], in_=pcb[b][:]),
                  r=[("pcb", b)], w=[("wbscr", id(Wbs), it, q)], dma=True)
            yield

    precast = precast_gen() if PRECAST else iter(())
    tile_ctr = {"n": 0}
    gate_pending = []

    def pc_steps(n):
        for _ in range(n):
            try:
                next(precast)
            except StopIteration:
                return

    for si, (t0s, T, Lp) in enumerate(SEQS):
        tt = min(128, T)
        ntile = T // tt
        if si == 0:
            S.add("gpsimd", lambda e: e.memset(Sst[:], 0.0), w=["Sst"])
            S.add("gpsimd", lambda e: e.memset(Sbf[:], 0.0), w=["Sbf"])
        else:
            S.add("sync", lambda e: e.dma_start(out=Sst[:].rearrange("p a b -> p (a b)"), in_=s0T), w=["Sst"], dma=True)
            S.add("gpsimd", lambda e: e.tensor_copy(out=Sbf[:], in_=Sst[:]), r=["Sst"], w=["Sbf"])
        for ti in range(ntile):
            t0 = t0s + ti * tt
            cok = [("co", c) for c in range(32)]
            S.add("sync", lambda e, t0=t0, tt=tt: e.dma_start(out=sb16[:, :, 0:tt], in_=xbcTbv[:, :, t0:t0 + tt]),
                  r=[("scr", "xbcb", r) for r in range(0, CD, 128)], w=["sb16"], dma=True)
            S.add("sync", lambda e, t0=t0, tt=tt: e.dma_start(out=co[:, :, 0:tt], in_=xbcTv[:, 0:32, t0:t0 + tt]),
                  r=[("scr", "xbc", r) for r in range(0, DI, 128)], w=cok, dma=True)
            for g4 in range(10):
                bk = mbank()
                for j in range(4):
                    c = g4 * 4 + j
                    S.add("tensor", lambda e, bk=bk, j=j, c=c, tt=tt: e.matmul(
                        PS[bk][0:tt, j * 128:(j + 1) * 128], lhsT=sb16[:, c, 0:tt], rhs=identb[:], start=True, stop=True),
                        r=["sb16", "identb"], w=[("ps", bk)])
                eng = evac_eng()
                if g4 < 8:
                    dst_ap = xs_tm[0:tt, g4 * 8:(g4 + 1) * 8, :].rearrange("p a b -> p (a b)")
                    S.add(eng, copy_op(eng, dst_ap, PS[bk][0:tt, :]), r=[("ps", bk)], w=[("xs_tm", g4)])
                else:
                    S.add(eng, copy_op(eng, B_tm[0:tt, (g4 - 8) * 4:(g4 - 8) * 4 + 4, :].rearrange("p a b -> p (a b)"),
                                       PS[bk][0:tt, :]), r=[("ps", bk)], w=[("B_tm", g4 - 8)])
            xsk = [("xs_tm", g) for g in range(8)]
            S.add("sync", lambda e, t0=t0, tt=tt: e.dma_start(out=dd[0:64, 0:tt], in_=dtT[:, t0:t0 + tt]),
                  r=[("scr", "dt", 0)], w=["dd0", ("Gm", 0), ("Gm", 1)], dma=True)
            S.add("sync", lambda e, t0=t0, tt=tt: e.dma_start(out=dd[64:128, 0:tt], in_=dtT[:, t0:t0 + tt]),
                  r=[("scr", "dt", 0)], w=["dd1", ("Gm", 0), ("Gm", 1)], dma=True)
            S.add("scalar", lambda e, tt=tt: e.activation(out=dd[:, 0:tt], in_=dd[:, 0:tt], func=AF.Exp, bias=pv(PV_DTB)),
                  r=["dd0", "dd1", "pvec"], w=["dd"])
            S.add("scalar", lambda e, tt=tt: e.activation(out=dd[:, 0:tt], in_=dd[:, 0:tt], func=AF.Ln, bias=1.0),
                  r=["dd"], w=["dd"])
            S.add("vector", lambda e, tt=tt: e.tensor_scalar(out=dd[64:128, 0:tt], in0=dd[64:128, 0:tt],
                                                             scalar1=pvec[64:128, PV_ALOG:PV_ALOG + 1], scalar2=None,
                                                             op0=ALU.mult), r=["dd", "pvec"], w=["dd"])
            bk = mbank()
            S.add("tensor", lambda e, bk=bk, tt=tt: e.matmul(PS[bk][0:tt, 0:128], lhsT=dd[:, 0:tt], rhs=ident, start=True, stop=True),
                  r=["dd", "consts"], w=[("ps", bk)])
            S.add("vector", copy_op("vector", ddtm[0:tt, :], PS[bk][0:tt, 0:128]), r=[("ps", bk)], w=["ddtm"])
            S.add("vector", lambda e, tt=tt: e.tensor_copy(out=dAhl[0:tt, 0, :], in_=ddtm[0:tt, 64:128]), r=["ddtm"], w=["dAh"])
            S.add("vector", lambda e, tt=tt: e.tensor_tensor(out=dAhl[0:tt, 1, :], in0=ddtm[0:tt, 64:128], in1=dAhl[0:tt, 0, :],
                                                             op=ALU.subtract), r=["ddtm", "dAh"], w=["dAl"])
            bk = mbank()
            S.add("tensor", lambda e, bk=bk, tt=tt: e.matmul(PS[bk][0:tt, 0:64], lhsT=LE[0:tt, 0:tt], rhs=ddtm[0:tt, 64:128],
                                                             start=True, stop=True), r=["ddtm", "consts"], w=[("ps", bk)])
            S.add("tensor", lambda e, bk=bk, tt=tt: e.matmul(PS[bk][0:128, 64:128], lhsT=ones[0:tt, 0:128], rhs=ddtm[0:tt, 64:128],
                                                             start=True, stop=True), r=["ddtm", "consts"], w=[("ps", bk)])
            S.add("scalar", copy_op("scalar", small[:, 0, :], PS[bk][:, 64:128]), r=[("ps", bk)], w=["tot"])
            S.add("vector", lambda e, bk=bk, tt=tt: e.tensor_tensor(out=small[0:tt, 1, :], in0=small[0:tt, 0, :],
                                                                    in1=PS[bk][0:tt, 0:64], op=ALU.subtract),
                  r=["tot", ("ps", bk)], w=["tmpd"])
            S.add("scalar", lambda e, tt=tt: e.activation(out=small[0:tt, 2, :], in_=small[0:tt, 1, :], func=AF.Exp),
                  r=["tmpd"], w=["decay"])
            S.add("scalar", lambda e: e.activation(out=small[:, 3, :], in_=small[:, 0, :], func=AF.Exp), r=["tot"], w=["ecl"])
            S.add("vector", lambda e, tt=tt: e.tensor_tensor(out=small[0:tt, 4, :], in0=small[0:tt, 2, :], in1=ddtm[0:tt, 0:64],
                                                             op=ALU.mult), r=["decay", "ddtm"], w=["dtdec"])
            S.add("vector", lambda e, tt=tt: e.tensor_tensor(
                out=xd[0:tt], in0=xs_tm[0:tt], in1=ddtm[0:tt, 0:64].unsqueeze(2).broadcast_to([tt, 64, 64]), op=ALU.mult),
                r=xsk + ["ddtm"], w=["xd"])
            S.add("vector", lambda e, tt=tt: e.tensor_tensor(
                out=xdd[0:tt], in0=xs_tm[0:tt], in1=small[0:tt, 4, :].unsqueeze(2).broadcast_to([tt, 64, 64]), op=ALU.mult),
                r=xsk + ["dtdec"], w=["xdd"])
            for g4 in range(2):
                bk = mbank()
                for j in range(4):
                    g = g4 * 4 + j
                    S.add("tensor", lambda e, bk=bk, j=j, g=g, tt=tt: e.matmul(
                        PS[bk][0:tt, j * 128:j * 128 + tt], lhsT=sb16[:, 32 + g, 0:tt], rhs=sb16[:, 40 + g, 0:tt],
                        start=True, stop=True), r=["sb16"], w=[("ps", bk)])
                S.add("vector", lambda e, bk=bk, g4=g4, tt=tt: e.tensor_tensor(
                    out=Gm[0:tt, g4 * 4:g4 * 4 + 4, 0:tt],
                    in0=PS[bk][0:tt, :].rearrange("p (a b) -> p a b", b=128)[:, :, 0:tt],
                    in1=LE[0:tt, 0:tt].unsqueeze(1).broadcast_to([tt, 4, tt]), op=ALU.mult),
                    r=[("ps", bk), "consts"], w=[("Gm", g4), "dd", "dd0", "dd1"])
            par = tile_ctr["n"] % 2
            tile_ctr["n"] += 1

            def grp_task(g, bi, tt=tt, par=par):
                (rDh, rDl), eD, eC = rhsDs[bi], expDs[bi], ecsbs[bi]
                for x_, rD_ in ((0, rDh), (1, rDl)):
                    eng = vp()
                    S.add(eng, lambda e, x_=x_, rD_=rD_: e.tensor_tensor(
                        out=rD_[0:tt, :, 0:tt],
                        in0=dAhl[0:tt, x_, g * 8:g * 8 + 8].unsqueeze(2).broadcast_to([tt, 8, tt]),
                        in1=LEb[0:tt, 0:tt].unsqueeze(1).broadcast_to([tt, 8, tt]), op=ALU.mult),
                        r=["dAh", "dAl", "constsb"], w=[("rhsD", bi, x_)])
                yield
                hpm = min(8, 512 // tt)
                for m in range(8 // hpm):
                    h0 = m * hpm
                    bk = mbank()
                    for x_, rD_ in ((0, rDh), (1, rDl)):
                        S.add("tensor", lambda e, bk=bk, h0=h0, x_=x_, rD_=rD_: e.matmul(
                            PS[bk][0:tt, 0:hpm * tt].rearrange("p (a b) -> p a b", b=tt), lhsT=Ub[0:tt, 0:tt],
                            rhs=rD_[0:tt, h0:h0 + hpm, 0:tt], start=(x_ == 0), stop=(x_ == 1)),
                            r=[("rhsD", bi, 0), ("rhsD", bi, 1), "constsb"], w=[("ps", bk)])
                    S.add("scalar", lambda e, bk=bk, h0=h0: e.activation(
                        out=eD[0:tt, h0:h0 + hpm, 0:tt], in_=PS[bk][0:tt, 0:hpm * tt].rearrange("p (a b) -> p a b", b=tt),
                        func=AF.Exp), r=[("ps", bk)], w=[("expD", bi, m)])
                    yield
                    bk = mbank()
                    for x_, rD_ in ((0, rDh), (1, rDl)):
                        S.add("tensor", lambda e, bk=bk, h0=h0, x_=x_, rD_=rD_: e.matmul(
                            PS[bk][0:128, 0:hpm * tt].rearrange("p (a b) -> p a b", b=tt), lhsT=onesb[0:tt, 0:128],
                            rhs=rD_[0:tt, h0:h0 + hpm, 0:tt], start=(x_ == 0), stop=(x_ == 1)),
                            r=[("rhsD", bi, 0), ("rhsD", bi, 1), "constsb"], w=[("ps", bk)])
                    S.add("scalar", lambda e, bk=bk, h0=h0: e.activation(
                        out=eC[:, h0:h0 + hpm, 0:tt], in_=PS[bk][:, 0:hpm * tt].rearrange("p (a b) -> p a b", b=tt),
                        func=AF.Exp), r=[("ps", bk)], w=[("ecsb", bi, m)])
                    yield
                edk = [("expD", bi, m) for m in range(8 // hpm)]
                eck = [("ecsb", bi, m) for m in range(8 // hpm)]
                eng = vp()
                S.add(eng, lambda e: e.tensor_tensor(
                    out=eD[0:tt, :, 0:tt], in0=eD[0:tt, :, 0:tt], in1=Gm[0:tt, g:g + 1, 0:tt].broadcast_to([tt, 8, tt]),
                    op=ALU.mult), r=edk + [("Gm", g // 4)], w=edk)
                yield
                eng = vp()
                S.add(eng, lambda e: e.tensor_tensor(
                    out=eC[:, :, 0:tt], in0=eC[:, :, 0:tt], in1=sb16[:, 40 + g:41 + g, 0:tt].broadcast_to([128, 8, tt]),
                    op=ALU.mult), r=eck + ["sb16"], w=eck)
                yield
                bk = mbank()
                for pj in range(4):
                    for hx in range(2):
                        hl = pj * 2 + hx
                        h = g * 8 + hl
                        S.add("tensor", lambda e, bk=bk, pj=pj, hx=hx, hl=hl, h=h: e.matmul(
                            PS[bk][hx * 64:(hx + 1) * 64, pj * 128:pj * 128 + tt], lhsT=xd[0:tt, h, :],
                            rhs=eD[0:tt, hl, 0:tt], start=True, stop=False), r=["xd"] + edk, w=[("ps", bk)])
                        S.add("tensor", lambda e, bk=bk, pj=pj, hx=hx, hl=hl, h=h: e.matmul(
                            PS[bk][hx * 64:(hx + 1) * 64, pj * 128:pj * 128 + tt], lhsT=Sbf[:, h, :],
                            rhs=eC[:, hl, 0:tt], start=False, stop=True), r=["Sbf"] + eck, w=[("ps", bk)])
                    yield
                for pj in range(4):
                    c = g * 4 + pj
                    S.add("vector", lambda e, bk=bk, pj=pj, c=c: e.scalar_tensor_tensor(
                        out=yTs[par][:, c, 0:tt], in0=co[:, c, 0:tt], scalar=pvec[:, PV_DS + c:PV_DS + c + 1],
                        in1=PS[bk][:, pj * 128:pj * 128 + tt], op0=ALU.mult, op1=ALU.add),
                        r=[("co", c), "pvec", ("ps", bk)], w=[("yT", par, c)])
                yield

            gact = []
            if gate_pending:
                gact.append(("gate", gate_pending.pop(0)))
            for g in range(8):
                bi = g % 4
                while any(a[0] == bi for a in gact) or sum(1 for a in gact if a[0] != "gate") >= 3:
                    for a in list(gact):
                        try:
                            next(a[1])
                        except StopIteration:
                            gact.remove(a)
                gact.append((bi, grp_task(g, bi)))
                pc_steps(PC_PER)
            while gact:
                for a in list(gact):
                    try:
                        next(a[1])
                    except StopIteration:
                        gact.remove(a)
            for g in range(8):
                bk = mbank()
                S.add("tensor", lambda e, bk=bk, g=g, tt=tt: e.matmul(
                    PS[bk][:, 0:512], lhsT=B_tm[0:tt, g, :], rhs=xdd[0:tt, g * 8:(g + 1) * 8, :].rearrange("p a b -> p (a b)"), start=True, stop=True),
                    r=[("B_tm", 0), ("B_tm", 1), "xdd"], w=[("ps", bk)])
                S.add("gpsimd", lambda e, g=g: e.tensor_tensor(
                    out=Sst[:, g * 8:(g + 1) * 8, :], in0=Sst[:, g * 8:(g + 1) * 8, :],
                    in1=small[:, 3, g * 8:(g + 1) * 8].unsqueeze(2).broadcast_to([128, 8, 64]), op=ALU.mult),
                    r=["Sst", "ecl"], w=["Sst"])
                S.add("vector", lambda e, bk=bk, g=g: e.tensor_tensor(
                    out=Sst[:, g * 8:(g + 1) * 8, :], in0=Sst[:, g * 8:(g + 1) * 8, :],
                    in1=PS[bk][:, 0:512].rearrange("p (a b) -> p a b", b=64), op=ALU.add),
                    r=["Sst", ("ps", bk)], w=["Sst"])
            S.add("scalar", copy_op("scalar", Sbf[:], Sst[:]), r=["Sst"], w=["Sbf"])
            def gate_task(t0=t0, tt=tt, par=par):
                yTp, zb, rs, ynb = yTs[par], zbufs[par], rstds[par], ynbs[par]
                ytk = [("yT", par, c) for c in range(32)]
                zk_ = ("zbuf", par)
                S.add("sync", lambda e: e.dma_start(out=zb[:, :, 0:tt], in_=zTv[:, :, t0:t0 + tt]),
                      r=[("scr", "z", r) for r in range(0, DI, 128)], w=[zk_], dma=True)
                yield
                S.add("scalar", lambda e: e.activation(out=zb[:, :, 0:tt], in_=zb[:, :, 0:tt], func=AF.Silu), r=[zk_], w=[zk_])
                yield
                S.add("vector", lambda e: e.tensor_tensor(out=yTp[:, :, 0:tt], in0=yTp[:, :, 0:tt], in1=zb[:, :, 0:tt],
                                                          op=ALU.mult), r=ytk + [zk_], w=ytk)
                yield
                S.add("scalar", lambda e: e.activation(out=zb[:, :, 0:tt], in_=yTp[:, :, 0:tt], func=AF.Square), r=ytk, w=[zk_])
                yield
                for g4 in range(2):
                    bk = mbank()
                    for j in range(4):
                        g = g4 * 4 + j
                        for i4 in range(4):
                            S.add("tensor", lambda e, bk=bk, j=j, g=g, i4=i4: e.matmul(
                                PS[bk][:, j * 128:j * 128 + tt], lhsT=ones512, rhs=zb[:, g * 4 + i4, 0:tt],
                                start=(i4 == 0), stop=(i4 == 3)), r=[zk_, "consts"], w=[("ps", bk)])
                        yield
                    S.add("scalar", lambda e, bk=bk, g4=g4: e.activation(
                        out=rs[:, g4 * 4:g4 * 4 + 4, 0:tt], in_=PS[bk][:, :].rearrange("p (a b) -> p a b", b=128)[:, :, 0:tt],
                        func=AF.Ln, bias=RMS_EPS), r=[("ps", bk)], w=[("rstd", par, g4)])
                    yield
                    S.add("scalar", lambda e, g4=g4: e.activation(
                        out=rs[:, g4 * 4:g4 * 4 + 4, 0:tt], in_=rs[:, g4 * 4:g4 * 4 + 4, 0:tt], func=AF.Exp, scale=-0.5),
                        r=[("rstd", par, g4)], w=[("rstd", par, g4)])
                    yield
                for g in range(8):
                    S.add("vector", lambda e, g=g: e.tensor_tensor(
                        out=yTp[:, g * 4:g * 4 + 4, 0:tt], in0=yTp[:, g * 4:g * 4 + 4, 0:tt],
                        in1=rs[:, g:g + 1, 0:tt].broadcast_to([128, 4, tt]), op=ALU.mult),
                        r=ytk + [("rstd", par, g // 4)], w=[("yT", par, c) for c in range(g * 4, g * 4 + 4)])
                    yield
                S.add("vector", lambda e: e.tensor_tensor(
                    out=ynb[:, :, 0:tt], in0=yTp[:, :, 0:tt],
                    in1=pvec[:, PV_NW:PV_NW + 32].unsqueeze(2).broadcast_to([128, 32, tt]), op=ALU.mult),
                    r=ytk + ["pvec", zk_], w=[zk_])
                yield
                S.add("sync", lambda e: e.dma_start(out=ynTv[:, :, t0:t0 + tt], in_=ynb[:, :, 0:tt]),
                      r=[zk_], w=[("scr", "yn", t0)], dma=True)
                yield

            gate_pending.append(gate_task())
        while gate_pending:
            for _ in gate_pending.pop(0):
                pass
        for g4 in range(8):
            bk = mbank()
            for j in range(4):
                c = g4 * 4 + j
                S.add("tensor", lambda e, bk=bk, j=j, c=c: e.matmul(
                    PS[bk][:, j * 128:(j + 1) * 128], lhsT=Sst[:, 2 * c:2 * c + 2, :].rearrange("p a b -> p (a b)"), rhs=ident,
                    start=True, stop=True),
                    r=["Sst", "consts"], w=[("ps", bk)])
            eng = evac_eng()
            S.add(eng, copy_op(eng, yT[:, g4 * 4:g4 * 4 + 4, :], PS[bk][:, :].rearrange("p (a b) -> p a b", b=128)),
                  r=[("ps", bk)], w=[("yT", 0, c) for c in range(g4 * 4, g4 * 4 + 4)])
        S.add("sync", lambda e, si=si: e.dma_start(out=ssm_o[si].rearrange("(c p) n -> p c n", p=128), in_=yT[:]),
              r=[("yT", 0, c) for c in range(32)], w=[("out", "ssm", si)], dma=True)

    pc_steps(1 << 30)
    state["mb_lo"], state["mb_n"] = 4, 4
    if stop == "B":
        S.emit(nc)
        return nc
    S.barrier()
    A.reset(base_mark)

    NKM = max(TP, LP + TS)
    NSLOT = 5
    NVT = max(TP // 128, LP // 128 + 1)
    KThs = [A.alloc([128, NKM], BF16) for _ in range(2)]
    QThs = [A.alloc([128, max(TP, TS)], BF16) for _ in range(2)]
    Vhs = [A.alloc([128, NVT, 128], BF16) for _ in range(2)]
    ysts = [A.alloc([128, max(TP, TS)], BF16) for _ in range(2)]
    kst = A.alloc([128, LP], F32)
    vst = A.alloc([128, LP // 128, 128], F32)
    zss = [A.alloc([128, NKM], F32) for _ in range(NSLOT)]
    sps = [A.alloc([128, NKM], F32) for _ in range(NSLOT)]
    Pcs = [A.alloc([128, NKM + 1], F32) for _ in range(NSLOT)]
    Wbs = [A.alloc([128, NKM], BF16) for _ in range(NSLOT)]
    WTs = [A.alloc([128, NKM // 128 + 1, 128], BF16) for _ in range(NSLOT)]
    ntots = [A.alloc([128, 2], F32) for _ in range(NSLOT)]
    for sl in range(NSLOT):
        S.add("gpsimd", lambda e, sl=sl: e.memset(Pcs[sl][:, 0:1], 0.0), w=[("Pc0", sl)])

    def head_loads(si, h, hb):
        t0s, T, Lp = SEQS[si]
        KTh, QTh, Vh = KThs[hb], QThs[hb], Vhs[hb]
        rows = slice(h * 128, (h + 1) * 128)
        if Lp:
            S.add("sync", lambda e: e.dma_start(out=kst[:], in_=ckT[h]), w=["kst"], dma=True)
            S.add("gpsimd", lambda e: e.tensor_copy(out=KTh[:, 0:LP], in_=kst[:]), r=["kst"], w=[("KTh_p", hb), ("KTh", hb)])
            S.add("sync", lambda e: e.dma_start(out=vst[:], in_=cv[:, rows].rearrange("(n p) f -> p n f", p=128)),
                  w=["vst"], dma=True)
            S.add("gpsimd", lambda e: e.tensor_copy(out=Vh[:, 0:LP // 128, :], in_=vst[:]), r=["vst"],
                  w=[("Vh_p", hb), ("Vh", hb)])
        S.add("sync", lambda e: e.dma_start(out=KTh[:, Lp:Lp + T], in_=kT[rows, t0s:t0s + T]),
              r=[("scr", "k", h * 128)], w=[("KTh", hb)], dma=True)
        S.add("sync", lambda e: e.dma_start(out=QTh[:, 0:T], in_=qT[rows, t0s:t0s + T]),
              r=[("scr", "q", h * 128)], w=[("QTh", hb)], dma=True)
        if T >= 128:
            S.add("sync", lambda e: e.dma_start(
                out=Vh[:, Lp // 128:Lp // 128 + T // 128, :],
                in_=vtm[t0s:t0s + T, rows].rearrange("(n p) f -> p n f", p=128)),
                r=[("scr", "vtm", h, 0)], w=[("Vh", hb)], dma=True)
        else:
            S.add("sync", lambda e: e.dma_start(out=Vh[0:T, Lp // 128, :], in_=vtm[t0s:t0s + T, rows]),
                  r=[("scr", "vtm", h, 1)], w=[("Vh", hb)], dma=True)

    def attn_task(si, h, hb, qi, sl, last):
        t0s, T, Lp = SEQS[si]
        tt = min(128, T)
        nq = T // tt
        KTh, QTh, Vh, yst = KThs[hb], QThs[hb], Vhs[hb], ysts[hb]
        zs, sp, Pc, Wb, WT, ntot = zss[sl], sps[sl], Pcs[sl], Wbs[sl], WTs[sl], ntots[sl]
        rows = slice(h * 128, (h + 1) * 128)
        q0 = qi * tt
        nk = Lp + (qi + 1) * tt
        for k0 in range(0, nk, 512):
            n = min(512, nk - k0)
            state["gbank"] = (state["gbank"] + 1) % 4
            bk = state["gbank"]
            S.add("tensor", lambda e, bk=bk, k0=k0, n=n: e.matmul(
                PS[bk][0:tt, 0:n], lhsT=QTh[:, q0:q0 + tt], rhs=KTh[:, k0:k0 + n], start=True, stop=True),
                r=[("QTh", hb), ("KTh", hb), ("KTh_p", hb)], w=[("ps", bk)])
            eng = evac_eng()
            if eng == "scalar":
                S.add(eng, lambda e, bk=bk, k0=k0, n=n: e.activation(
                    out=zs[0:tt, k0:k0 + n], in_=PS[bk][0:tt, 0:n], func=AF.Copy, scale=SB_SCALE),
                    r=[("ps", bk)], w=[("zs", sl, k0)])
            else:
                S.add(eng, lambda e, bk=bk, k0=k0, n=n: e.tensor_scalar(
                    out=zs[0:tt, k0:k0 + n], in0=PS[bk][0:tt, 0:n], scalar1=SB_SCALE, scalar2=None, op0=ALU.mult),
                    r=[("ps", bk)], w=[("zs", sl, k0)])
            yield
        zk = [("zs", sl, k0) for k0 in range(0, nk, 512)]
        spk = ("sp", sl)
        S.add("scalar", lambda e: e.activation(out=sp[0:tt, 0:nk], in_=zs[0:tt, 0:nk], func=AF.Exp), r=zk, w=[spk])
        yield
        S.add("scalar", lambda e: e.activation(out=sp[0:tt, 0:nk], in_=sp[0:tt, 0:nk], func=AF.Ln, bias=1.0), r=[spk], w=[spk])
        yield
        S.add("gpsimd", lambda e: e.tensor_tensor(out=sp[0:tt, nk - tt:nk], in0=sp[0:tt, nk - tt:nk], in1=Umat[0:tt, 0:tt],
                                                  op=ALU.mult), r=[spk, "consts"], w=[spk])
        yield
        S.add("vector", lambda e: e.tensor_tensor_scan(out=Pc[0:tt, 1:nk + 1], data0=sp[0:tt, 0:nk], data1=sp[0:tt, 0:nk],
                                                       initial=0.0, op0=ALU.add, op1=ALU.max),
              r=[spk, ("Pc0", sl)], w=[("Pc", sl)])
        yield
        S.add("vector", lambda e: e.tensor_scalar(out=ntot[0:tt, 0:1], in0=Pc[0:tt, nk:nk + 1], scalar1=-1.0, scalar2=None,
                                                  op0=ALU.mult), r=[("Pc", sl)], w=[("ntot", sl)])
        yield
        S.add("gpsimd", lambda e: e.tensor_tensor(out=zs[0:tt, 0:nk], in0=zs[0:tt, 0:nk], in1=Pc[0:tt, 0:nk], op=ALU.add),
              r=zk + [("Pc", sl), ("Pc0", sl)], w=zk)
        yield
        S.add("scalar", lambda e: e.activation(out=Wb[0:tt, 0:nk], in_=zs[0:tt, 0:nk], func=AF.Exp, bias=ntot[0:tt, 0:1]),
              r=zk + [("ntot", sl)], w=[("Wb", sl)])
        yield
        S.add("gpsimd", lambda e: e.tensor_tensor(out=Wb[0:tt, nk - tt:nk], in0=Wb[0:tt, nk - tt:nk], in1=Umat[0:tt, 0:tt],
                                                  op=ALU.mult), r=[("Wb", sl), "consts"], w=[("Wb", sl)])
        yield
        kbs = [(k0, min(128, nk - k0)) for k0 in range(0, nk, 128)]
        for g8 in range(0, len(kbs), 4):
            bk = mbank()
            grp = kbs[g8:g8 + 4]
            for j, (k0, kw) in enumerate(grp):
                S.add("tensor", lambda e, bk=bk, j=j, k0=k0, kw=kw: e.matmul(
                    PS[bk][0:kw, j * 128:j * 128 + tt], lhsT=Wb[0:tt, k0:k0 + kw], rhs=identb[0:tt, 0:tt],
                    start=True, stop=True), r=[("Wb", sl), "identb"], w=[("ps", bk)])
            eng = evac_eng()
            full = [x for x in grp if x[1] == 128]
            if full:
                nf = len(full)
                S.add(eng, copy_op(eng, WT[:, g8:g8 + nf, 0:tt],
                                   PS[bk][:, 0:nf * 128].rearrange("p (a b) -> p a b", b=128)[:, :, 0:tt]),
                      r=[("ps", bk)], w=[("WT", sl, g8)])
            if len(full) < len(grp):
                j = len(full)
                kw = grp[j][1]
                S.add(eng, copy_op(eng, WT[0:kw, g8 + j, 0:tt], PS[bk][0:kw, j * 128:j * 128 + tt]),
                      r=[("ps", bk)], w=[("WT", sl, g8, "s")])
            yield
        wtk = [("WT", sl, g8) for g8 in range(0, len(kbs), 4)] + [("WT", sl, g8, "s") for g8 in range(0, len(kbs), 4)]
        bk = mbank()
        for j, (k0, kw) in enumerate(kbs):
            S.add("tensor", lambda e, bk=bk, j=j, kw=kw, nkb=len(kbs): e.matmul(
                PS[bk][:, 0:tt], lhsT=Vh[0:kw, j, :], rhs=WT[0:kw, j, 0:tt], start=(j == 0), stop=(j == nkb - 1)),
                r=wtk + [("Vh", hb), ("Vh_p", hb)], w=[("ps", bk)])
        eng = evac_eng()
        S.add(eng, copy_op(eng, yst[:, q0:q0 + tt], PS[bk][:, 0:tt]), r=[("ps", bk)], w=[("yst", hb, qi)])
        yield
        if last:
            S.add("sync", lambda e: e.dma_start(out=ysbT[rows, t0s:t0s + T], in_=yst[:, 0:T]),
                  r=[("yst", hb, q) for q in range(nq)], w=[("scr", "ysb", h, si)], dma=True)
            yield

    def attn_tasks():
        hc = 0
        tc = 0
        for si, (t0s, T, Lp) in enumerate(SEQS):
            tt = min(128, T)
            nq = T // tt
            for h in range(AH):
                hb = hc % 2
                hc += 1
                for qi in range(nq):
                    yield (si, h, hb, qi, tc % NSLOT, qi == nq - 1, qi == 0)
                    tc += 1

    active = []
    for (si, h, hb, qi, sl, last, first) in attn_tasks():
        if first:
            while any(a[2] == hb for a in active):
                for a in list(active):
                    try:
                        next(a[1])
                    except StopIteration:
                        active.remove(a)
            head_loads(si, h, hb)
        while any(a[0] == sl for a in active) or len(active) >= NSLOT:
            for a in list(active):
                try:
                    next(a[1])
                except StopIteration:
                    active.remove(a)
        active.append((sl, attn_task(si, h, hb, qi, sl, last), hb))
    while active:
        for a in list(active):
            try:
                next(a[1])
            except StopIteration:
                active.remove(a)

    if stop == "C":
        S.emit(nc)
        return nc
    S.barrier()
    A.reset(base_mark)

    wst = [A.alloc([128, 16, 256], F32) for _ in range(2)]
    wbf = [A.alloc([128, 16, 256], BF16) for _ in range(2)]
    P1 = A.mark()
    ynB = A.alloc([128, 32, 512], BF16)
    ysB = A.alloc([128, 16, 512], BF16)
    MGb = A.alloc([128, 16, 512], BF16)
    Hb = A.alloc([128, 64, 512], BF16, at=P1)
    MG = A.alloc([128, 16, 512], F32)
    X1b = A.alloc([128, 16, 512], BF16, at=A.off - 16 * 512 * 4)
    R1 = A.alloc([128, 16, 512], F32)
    gst = [A.alloc([128, 512], F32) for _ in range(2)]
    tmpd = A.alloc([128, 512], F32)
    tmpb = A.alloc([128, 512], F32)
    stat = A.alloc([128, 2, 512], F32)
    yout = A.alloc([128, 4, 512], F32)
    ynTv = ynT.rearrange("(c p) t -> p c t", p=128)
    ysbTv = ysbT.rearrange("(c p) t -> p c t", p=128)
    gctr = {"n": 0}

    def layer_norm(nt, gcol, bcol, make_bf):
        r1k = [("R1", c) for c in range(16)]
        bk = mbank()
        for c in range(16):
            S.add("tensor", lambda e, bk=bk, c=c: e.matmul(PS[bk][:, 0:nt], lhsT=ones, rhs=R1[:, c, 0:nt],
                                                           start=(c == 0), stop=(c == 15)), r=r1k + ["consts"], w=[("ps", bk)])
        S.add("scalar", lambda e, bk=bk: e.activation(out=stat[:, 0, 0:nt], in_=PS[bk][:, 0:nt], func=AF.Copy,
                                                      scale=1.0 / D), r=[("ps", bk)], w=["mean"])
        for c in range(16):
            eng = vp()
            S.add(eng, lambda e, c=c: e.tensor_tensor(out=R1[:, c, 0:nt], in0=R1[:, c, 0:nt], in1=stat[:, 0, 0:nt],
                                                      op=ALU.subtract), r=[("R1", c), "mean"], w=[("R1", c)])
        bk = mbank()
        for c in range(16):
            sqb = gst[c % 2]
            S.add("scalar", lambda e, c=c, sqb=sqb: e.activation(out=sqb[:, 0:nt], in_=R1[:, c, 0:nt], func=AF.Square),
                  r=[("R1", c)], w=[("gst", c % 2)])
            S.add("tensor", lambda e, bk=bk, c=c, sqb=sqb: e.matmul(PS[bk][:, 0:nt], lhsT=ones, rhs=sqb[:, 0:nt],
                                                                    start=(c == 0), stop=(c == 15)),
                  r=[("gst", c % 2), "consts"], w=[("ps", bk)])
        S.add("scalar", lambda e, bk=bk: e.activation(out=stat[:, 1, 0:nt], in_=PS[bk][:, 0:nt], func=AF.Sqrt,
                                                      scale=1.0 / D, bias=LN_EPS), r=[("ps", bk)], w=["rstd"])
        S.add("vector", lambda e: e.reciprocal(out=stat[:, 1, 0:nt], in_=stat[:, 1, 0:nt]), r=["rstd"], w=["rstd"])
        for c in range(16):
            eng = vp()
            S.add(eng, lambda e, c=c: e.tensor_tensor(out=R1[:, c, 0:nt], in0=R1[:, c, 0:nt], in1=stat[:, 1, 0:nt],
                                                      op=ALU.mult), r=[("R1", c), "rstd"], w=[("R1", c)])
            S.add("vector", lambda e, c=c: e.tensor_scalar(out=R1[:, c, 0:nt], in0=R1[:, c, 0:nt],
                                                           scalar1=pvec[:, gcol + c:gcol + c + 1],
                                                           scalar2=pvec[:, bcol + c:bcol + c + 1], op0=ALU.mult, op1=ALU.add),
                  r=[("R1", c), "pvec"], w=[("R1", c)])
            if make_bf:
                S.add("gpsimd", lambda e, c=c: e.tensor_copy(out=X1b[:, c, 0:nt], in_=R1[:, c, 0:nt]), r=[("R1", c)],
                      w=[("X1b", c)] + [("MG", cc) for cc in range(8)])

    gTv = gT.rearrange("(c p) t -> p c t", p=128)
    hbk = [("Hb", c) for c in range(64)]
    mgk = [("MG", c) for c in range(16)]
    x1k = [("X1b", c) for c in range(16)]

    def phase_d_block(tbi, t0, nt):
        wmode = "bf16" if (PRECAST or tbi > 0) else "cast_store"
        S.add("sync", lambda e: e.dma_start(out=ynB[:, :, 0:nt], in_=ynTv[:, :, t0:t0 + nt]),
              r=[("scr", "yn", t) for t in range(t0, t0 + nt, 128 if nt >= 128 else nt)], w=["ynB"] + hbk, dma=True)
        S.add("sync", lambda e: e.dma_start(out=ysB[:, :, 0:nt], in_=ysbTv[:, :, t0:t0 + nt]),
              r=[("scr", "ysb", h, 0 if t0 < TP else 1) for h in range(AH)], w=["ysB"] + hbk, dma=True)

        def gload(c, half, src=None):
            gctr["n"] += 1
            b = gctr["n"] % 2
            if src is None:
                S.add("sync", lambda e: e.dma_start(out=gst[b][:, 0:nt], in_=gTv[:, half * 16 + c, t0:t0 + nt]),
                      r=[("scr", "g", (half * 16 + c) * 128)], w=[("gst", b)], dma=True)
            else:
                S.add("sync", lambda e: e.dma_start(out=gst[b][:, 0:nt], in_=src), w=[("gst", b)], dma=True)
            return b

        def ev1(col, n, tbi_, nt_, ps, pskey):
            c = col // 128
            b = gload(c, 0)
            S.add("vector", lambda e: e.tensor_tensor(out=MG[:, c, 0:nt], in0=ps, in1=gst[b][:, 0:nt], op=ALU.mult),
                  r=[pskey, ("gst", b)], w=[("MG", c)] + (x1k if c < 8 else []))
        gemm(w_br_ssd, 0, DI, sbs_range(0, D), [(tbi, nt)], lambda kc, t: ynB[:, kc, 0:nt], lambda kc, t: ["ynB"], ev1, wst, wbf, wmode, wb_scr["ssd"])

        def ev2(col, n, tbi_, nt_, ps, pskey):
            c = col // 128
            b = gload(c, 1)
            S.add("vector", lambda e: e.tensor_tensor(out=tmpd[:, 0:nt], in0=ps, in1=gst[b][:, 0:nt], op=ALU.mult),
                  r=[pskey, ("gst", b)], w=["tmpd"])
            S.add("gpsimd", lambda e: e.tensor_tensor(out=MGb[:, c, 0:nt], in0=MG[:, c, 0:nt], in1=tmpd[:, 0:nt], op=ALU.add),
                  r=[("MG", c), "tmpd"], w=[("MGb", c)] + hbk)
        gemm(w_br_attn, 0, D, sbs_range(0, D), [(tbi, nt)], lambda kc, t: ysB[:, kc, 0:nt], lambda kc, t: ["ysB"], ev2, wst, wbf, wmode, wb_scr["attn"])

        def ev3(col, n, tbi_, nt_, ps, pskey):
            c = col // 128
            b = gload(c, 0, src=xTv[:, c, t0:t0 + nt])
            S.add("vector", lambda e: e.scalar_tensor_tensor(out=R1[:, c, 0:nt], in0=gst[b][:, 0:nt], scalar=ALPHA, in1=ps,
                                                             op0=ALU.mult, op1=ALU.add),
                  r=[pskey, ("gst", b)], w=[("R1", c)])
        gemm(w_out, 0, D, sbs_range(0, D), [(tbi, nt)], lambda kc, t: MGb[:, kc, 0:nt], lambda kc, t: [("MGb", kc)], ev3, wst, wbf, wmode, wb_scr["out"])
        layer_norm(nt, PV_L1G, PV_L1B, True)

        def ev4(col, n, tbi_, nt_, ps, pskey):
            c = col // 128
            S.add("scalar", lambda e: e.activation(out=tmpb[:, 0:nt], in_=ps, func=AF.Relu, bias=pvec[:, PV_BU + c:PV_BU + c + 1]),
                  r=[pskey, "pvec"], w=["tmpb"])
            S.add("vector", lambda e: e.tensor_tensor(out=Hb[:, c, 0:nt], in0=tmpb[:, 0:nt], in1=tmpb[:, 0:nt], op=ALU.mult),
                  r=["tmpb"], w=[("Hb", c), "ynB", "ysB"] + [("MGb", cc) for cc in range(16)])
        gemm(w_up, 0, D, sbs_range(0, DFF), [(tbi, nt)], lambda kc, t: X1b[:, kc, 0:nt], lambda kc, t: [("X1b", kc)], ev4, wst, wbf, wmode, wb_scr["up"])

        def ev5(col, n, tbi_, nt_, ps, pskey):
            c = col // 128
            S.add("gpsimd", lambda e: e.tensor_scalar(out=R1[:, c, 0:nt], in0=R1[:, c, 0:nt], scalar1=ALPHA,
                                                      scalar2=pvec[:, PV_BD + c:PV_BD + c + 1], op0=ALU.mult, op1=ALU.add),
                  r=[("R1", c), "pvec"], w=[("R1", c)])
            S.add("vector", lambda e: e.tensor_tensor(out=R1[:, c, 0:nt], in0=R1[:, c, 0:nt], in1=ps, op=ALU.add),
                  r=[("R1", c), pskey], w=[("R1", c)])
        gemm(w_down, 0, DFF, sbs_range(0, D), [(tbi, nt)], lambda kc, t: Hb[:, kc, 0:nt], lambda kc, t: [("Hb", kc)], ev5, wst, wbf, wmode, wb_scr["down"])
        layer_norm(nt, PV_L2G, PV_L2B, False)

        ttl = [(i, min(128, nt - i)) for i in range(0, nt, 128)]
        for ti_, (o, tn) in enumerate(ttl):
            for c4 in range(4):
                bk = mbank()
                for j in range(4):
                    c = c4 * 4 + j
                    S.add("tensor", lambda e, bk=bk, j=j, c=c, o=o, tn=tn: e.matmul(
                        PS[bk][0:tn, j * 128:(j + 1) * 128], lhsT=R1[:, c, o:o + tn], rhs=ident, start=True, stop=True),
                        r=[("R1", c), "consts"], w=[("ps", bk)])
                eng = evac_eng()
                S.add(eng, copy_op(eng, yout[0:tn, c4, :], PS[bk][0:tn, :]), r=[("ps", bk)], w=[("yout", c4)])
            S.add("sync", lambda e, o=o, tn=tn, t0=t0: e.dma_start(
                out=y_o[t0 + o:t0 + o + tn, :], in_=yout[0:tn].rearrange("p a b -> p (a b)")),
                r=[("yout", c4) for c4 in range(4)], w=[("out", "y", t0 + o)], dma=True)

    for tbi, (t0, nt) in enumerate(NTB):
        phase_d_block(tbi, t0, nt)

    S.emit(nc)
    return nc


def host_inputs(b, TP, inp, tw):
    f = np.float32
    xp = inp["x_prompt"][b][:TP]
    xs = inp["x_sample"][b]
    xT = np.ascontiguousarray(np.concatenate([xp, xs], axis=0).T).astype(f)
    convp = np.ascontiguousarray(inp["cache_conv"][0, b].T.reshape(48, 128, 3).transpose(1, 0, 2)).astype(f)
    s0T = np.ascontiguousarray(inp["state_ssm"][0, b].reshape(NH * HP, NS).T).astype(f)
    ckT = np.ascontiguousarray(inp["cache_k"][0, b].transpose(1, 2, 0)).astype(f)
    cvv = np.ascontiguousarray(inp["cache_v"][0, b].reshape(LP, D)).astype(f)
    pvec = np.zeros((128, NPV), f)

    def col(v):
        return np.asarray(v, f).reshape(-1, 128).T

    pvec[:, PV_BG:PV_BG + 32] = col(inp["b_gate"][0])
    for j in range(4):
        pvec[:, PV_CW + 48 * j:PV_CW + 48 * (j + 1)] = col(inp["conv_w"][0, j])
    pvec[:, PV_CB:PV_CB + 48] = col(inp["conv_b"][0])
    pvec[:, PV_NW:PV_NW + 32] = col(inp["ssm_norm_w"][0])
    pvec[:, PV_L1G:PV_L1G + 16] = col(inp["ln1_g"][0])
    pvec[:, PV_L1B:PV_L1B + 16] = col(inp["ln1_b"][0])
    pvec[:, PV_L2G:PV_L2G + 16] = col(inp["ln2_g"][0])
    pvec[:, PV_L2B:PV_L2B + 16] = col(inp["ln2_b"][0])
    pvec[:, PV_BD:PV_BD + 16] = col(inp["b_down"][0])
    pvec[:, PV_BU:PV_BU + 64] = col(inp["b_up"][0])
    pvec[:, PV_DS:PV_DS + 32] = col(np.repeat(np.asarray(inp["d_skip"][0], f), HP))
    pvec[0:64, PV_DTB] = inp["dt_bias"][0]
    pvec[64:128, PV_DTB] = inp["dt_bias"][0]
    pvec[64:128, PV_ALOG] = inp["a_log"][0]
    r = np.arange(128)
    consts = np.concatenate([np.eye(128, dtype=f), np.ones((128, 128), f), (r[:, None] > r[None, :]).astype(f),
                             (r[:, None] <= r[None, :]).astype(f), np.full((128, 128), 1.0 / 512, f)], axis=1)
    m = {"xT": xT, "convp": convp, "s0T": s0T, "ckT": ckT, "cv": cvv, "pvec": pvec, "consts": consts}
    m.update(tw)
    return m


def _tile_w(W, ranges):
    K = W.shape[0]
    tiles = []
    for (c0, c1) in ranges:
        for col0 in range(c0, c1, 256):
            ncols = min(256, c1 - col0)
            for seg in range(K // 2048):
                blk = W[seg * 2048:(seg + 1) * 2048, col0:col0 + ncols].reshape(16, 128, ncols).transpose(1, 0, 2)
                t = np.zeros((128, 16, 256), np.float32)
                t[:, :, :ncols] = blk
                tiles.append(t.reshape(128, 4096))
    return np.ascontiguousarray(np.stack(tiles))


def tiled_weights(inp):
    out = {"w_in": _tile_w(np.asarray(inp["w_in"][0], np.float32), WIN_PLAN)}
    for wn in ("w_br_ssd", "w_br_attn", "w_out", "w_up", "w_down"):
        W = np.asarray(inp[wn][0], np.float32)
        out[wn] = _tile_w(W, [(0, W.shape[1])])
    return out


_NC_CACHE = {}


def run(inp, TP, cores, dbg=False):
    key = (TP, dbg)
    if key not in _NC_CACHE:
        _NC_CACHE[key] = build(TP, dbg)
    nc = _NC_CACHE[key]
    tw = tiled_weights(inp)
    in_maps = [host_inputs(b, TP, inp, tw) for b in cores]
    res = run_bass_kernel_spmd(nc, in_maps, core_ids=list(range(len(cores))))
    return res.results


def kernel(**inputs):
    inp = {k: np.asarray(v) for k, v in inputs.items()}
    TP = inp["x_prompt"].shape[1]
    B = inp["x_prompt"].shape[0]
    res = run(inp, TP, list(range(B)))
    f = np.float32
    y_p = np.stack([r["y_o"][:TP] for r in res]).astype(f)
    y_s = np.stack([r["y_o"][TP:] for r in res]).astype(f)
    conv_p = np.stack([r["conv_p"] for r in res])[None].astype(f)
    conv_s = np.stack([r["conv_s"] for r in res])[None].astype(f)
    ssm_p = np.stack([r["ssm_p"].reshape(NH, HP, NS) for r in res])[None].astype(f)
    ssm_s = np.stack([r["ssm_s"].reshape(NH, HP, NS) for r in res])[None].astype(f)
    k_p = np.stack([r["k_o"][:TP].reshape(TP, AH, AD) for r in res])[None].astype(f)
    k_s = np.stack([r["k_o"][TP:].reshape(TS, AH, AD) for r in res])[None].astype(f)
    v_p = np.stack([r["v_o"][:TP].reshape(TP, AH, AD) for r in res])[None].astype(f)
    v_s = np.stack([r["v_o"][TP:].reshape(TS, AH, AD) for r in res])[None].astype(f)
    return (y_p, y_s, conv_p, ssm_p, k_p, v_p, conv_s, ssm_s, k_s, v_s)
```

```python
import numpy as np
import concourse.bass as bass
import concourse.mybir as mybir
from concourse.bass_utils import run_bass_kernel_spmd

F32 = mybir.dt.float32
BF16 = mybir.dt.bfloat16
ALU = mybir.AluOpType
AF = mybir.ActivationFunctionType

D = 2048
DI = 4096
NH = 64
HP = 64
NG = 8
NS = 128
CD = 6144
AH = 16
AD = 128
DFF = 8192
DIP = 20544
LP = 1024
TS = 64
ALPHA = 2.0 ** 0.25
LN_EPS = 1e-5
RMS_EPS = 1e-5
SB_SCALE = 1.0 / (128.0 ** 0.5)

PV_BG = 0
PV_CW = 32
PV_CB = PV_CW + 192
PV_NW = PV_CB + 48
PV_L1G = PV_NW + 32
PV_L1B = PV_L1G + 16
PV_L2G = PV_L1B + 16
PV_L2B = PV_L2G + 16
PV_BD = PV_L2B + 16
PV_BU = PV_BD + 16
PV_DS = PV_BU + 64
PV_DTB = PV_DS + 32
PV_ALOG = PV_DTB + 1
NPV = PV_ALOG + 1
XT_ENG = None
PRECAST = True
STQ = "scalar"
WIN_PLAN = [(0, 4096), (4096, 10240), (10240, 10304), (10304, 12352), (12352, 14400), (14400, 16448), (16448, 20544)]
WIN_ITEMS = sum((c1 - c0 + 255) // 256 for c0, c1 in WIN_PLAN)
CAST_ENG = None
EVAC_ENG = None


class _Op:
    __slots__ = ("eng", "fn", "deps", "dma", "flag", "sem", "val")


class Sched:
    ENGS = ("sync", "tensor", "vector", "scalar", "gpsimd")
    KDMA = 8

    def __init__(self):
        self.ops = []
        self.lastw = {}
        self.readers = {}
        self.fence = {}

    def add(self, eng, fn, r=(), w=(), dma=False):
        i = len(self.ops)
        deps = set()
        for k in r:
            j = self.lastw.get(k)
            if j is not None:
                deps.add(j)
        for k in w:
            j = self.lastw.get(k)
            if j is not None:
                deps.add(j)
            rs = self.readers.get(k)
            if rs:
                deps.update(rs)
        if eng in self.fence:
            deps |= self.fence.pop(eng)
        op = _Op()
        op.eng = eng
        op.fn = fn
        op.dma = dma
        op.flag = dma
        if eng == "tensor" and not dma:
            deps = {j for j in deps if not (self.ops[j].eng == "tensor" and not self.ops[j].dma)}
        op.deps = deps
        self.ops.append(op)
        for k in w:
            self.lastw[k] = i
            self.readers[k] = []
        for k in r:
            if k in self.lastw:
                self.readers[k].append(i)
            else:
                self.readers.setdefault(k, [])
        return i

    def barrier(self):
        last = {}
        dmas = set()
        for i, op in enumerate(self.ops):
            if op.dma:
                dmas.add(i)
            else:
                last[op.eng] = i
        per_q = {}
        for i in sorted(dmas):
            per_q.setdefault(self.ops[i].eng, []).append(i)
        f = set(last.values())
        for q, lst in per_q.items():
            f.update(lst[-self.KDMA:])
        for e in self.ENGS:
            self.fence[e] = set(f) | self.fence.get(e, set())

    def emit(self, nc):
        ops = self.ops
        for op in ops:
            for j in op.deps:
                ops[j].flag = True
        self.barrier()
        fin = self.add("sync", None)
        for j in ops[fin].deps:
            ops[j].flag = True
        import contextlib
        with contextlib.ExitStack() as st:
            csem = {e: st.enter_context(nc.semaphore("c_" + e)) for e in self.ENGS}
            dsem = {e: [st.enter_context(nc.semaphore("d_%s%d" % (e, k))) for k in range(self.KDMA)]
                    for e in ("sync", "gpsimd", "scalar")}
            ccount = {e: 0 for e in self.ENGS}
            dcount = {e: 0 for e in dsem}
            prev_dma = {}
            for i, op in enumerate(ops):
                if op.dma:
                    n = dcount[op.eng]
                    dcount[op.eng] += 1
                    op.sem = dsem[op.eng][n % self.KDMA]
                    op.val = 16 * (n // self.KDMA + 1)
                    if n >= self.KDMA:
                        op.deps.add(prev_dma[(op.eng, n - self.KDMA)])
                    prev_dma[(op.eng, n)] = i
                elif op.flag:
                    ccount[op.eng] += 1
                    op.sem = csem[op.eng]
                    op.val = ccount[op.eng]
            per_eng = {e: [i for i, op in enumerate(ops) if op.eng == e] for e in self.ENGS}
            block = st.enter_context(nc.Block())

            def run(e, name):
                waited = {}
                for i in per_eng[name]:
                    op = ops[i]
                    need = {}
                    for j in op.deps:
                        pj = ops[j]
                        key = id(pj.sem)
                        if key not in need or need[key][1] < pj.val:
                            need[key] = (pj.sem, pj.val)
                    for key, (sem, val) in need.items():
                        if waited.get(key, 0) >= val:
                            continue
                        waited[key] = val
                        e.wait_ge(sem, val)
                    if op.fn is None:
                        continue
                    ins = op.fn(e)
                    if op.dma:
                        ins.then_inc(op.sem, 16)
                    elif op.flag:
                        ins.then_inc(op.sem, 1)

            @block.sync
            def _(e):
                run(e, "sync")

            @block.tensor
            def _(e):
                run(e, "tensor")

            @block.vector
            def _(e):
                run(e, "vector")

            @block.scalar
            def _(e):
                run(e, "scalar")

            @block.gpsimd
            def _(e):
                run(e, "gpsimd")


class Arena:
    def __init__(self, nc, base=16512):
        self.nc = nc
        self.off = base
        self.n = 0

    def mark(self):
        return self.off

    def reset(self, off):
        self.off = off

    def alloc(self, shape, dt, at=None):
        nbytes = int(np.prod(shape[1:])) * (4 if dt == F32 else 2)
        nbytes = (nbytes + 63) // 64 * 64
        if at is None:
            at = self.off
            self.off += nbytes
        assert at + nbytes <= 229376, ("SBUF overflow", at, nbytes)
        self.n += 1
        return self.nc.alloc_sbuf_tensor_at("a%d" % self.n, list(shape), dt, offset=at)


def build(TP, dbg=False, stop=None):
    TT = TP + TS
    NTB = [(t, min(512, TP - t)) for t in range(0, TP, 512)] + [(TP, TS)]
    SEQS = [(0, TP, 0), (TP, TS, LP)]
    nc = bass.Bass("TRN2", target_bir_lowering=False)
    S = Sched()

    def din(name, shape, dt=F32):
        return nc.dram_tensor(name, list(shape), dt, kind="ExternalInput").ap()

    def dout(name, shape, dt=F32):
        return nc.dram_tensor(name, list(shape), dt, kind="ExternalOutput").ap()

    def dscr(name, shape, dt=F32):
        return nc.dram_tensor(name, list(shape), dt, kind="ExternalOutput" if dbg else "Internal").ap()

    xT = din("xT", [D, TT])
    convp = din("convp", [128, 48, 3])
    s0T = din("s0T", [128, NH * HP])
    ckT = din("ckT", [AH, 128, LP])
    cv = din("cv", [LP, D])
    w_in = din("w_in", [WIN_ITEMS, 128, 4096])
    w_br_ssd = din("w_br_ssd", [16, 128, 4096])
    w_br_attn = din("w_br_attn", [8, 128, 4096])
    w_out = din("w_out", [8, 128, 4096])
    w_up = din("w_up", [32, 128, 4096])
    w_down = din("w_down", [32, 128, 4096])
    wb_scr = {"ssd": dscr("wb_ssd", [16, 128, 4096], BF16), "attn": dscr("wb_attn", [8, 128, 4096], BF16),
              "out": dscr("wb_out", [8, 128, 4096], BF16), "up": dscr("wb_up", [32, 128, 4096], BF16),
              "down": dscr("wb_down", [32, 128, 4096], BF16)}
    pvec_d = din("pvec", [128, NPV])
    consts_d = din("consts", [128, 5 * 128])

    y_o = dout("y_o", [TT, D])
    conv_o = [dout("conv_p", [3, CD]), dout("conv_s", [3, CD])]
    ssm_o = [dout("ssm_p", [NH * HP, NS]), dout("ssm_s", [NH * HP, NS])]
    k_o = dout("k_o", [TT, D])
    v_o = dout("v_o", [TT, D])

    zT = dscr("zT", [DI, TT])
    xbcT = dscr("xbcT", [CD, TT])
    xbcTb = dscr("xbcTb", [CD, TT], BF16)
    dtT = dscr("dtT", [NH, TT])
    qT = dscr("qT", [D, TT], BF16)
    kT = dscr("kT", [D, TT], BF16)
    vtm = dscr("vtm", [TT, D], BF16)
    gT = dscr("gT", [DI, TT])
    ynT = dscr("ynT", [DI, TT], BF16)
    ysbT = dscr("ysbT", [D, TT], BF16)

    A = Arena(nc)
    PC_PER = max(1, -(-384 // (8 * max(1, TP // 128))))
    consts = A.alloc([128, 5 * 128], F32)
    pvec = A.alloc([128, NPV], F32)
    identb = A.alloc([128, 128], BF16)
    ident = consts[:, 0:128]
    ones = consts[:, 128:256]
    Umat = consts[:, 256:384]
    LE = consts[:, 384:512]
    ones512 = consts[:, 512:640]
    S.add("sync", lambda e: e.dma_start(out=consts[:], in_=consts_d), w=["consts"], dma=True)
    S.add("sync", lambda e: e.dma_start(out=pvec[:], in_=pvec_d), w=["pvec"], dma=True)
    S.add("vector", lambda e: e.tensor_copy(out=identb[:], in_=ident), r=["consts"], w=["identb"])
    constsb = A.alloc([128, 3 * 128], BF16)
    S.add("vector", lambda e: e.tensor_copy(out=constsb[:], in_=consts[:, 128:512]), r=["consts"], w=["constsb"])
    onesb = constsb[:, 0:128]
    Ub = constsb[:, 128:256]
    LEb = constsb[:, 256:384]
    S.add("scalar", lambda e: e.activation(out=pvec[64:128, PV_ALOG:PV_ALOG + 1], in_=pvec[64:128, PV_ALOG:PV_ALOG + 1],
                                           func=AF.Exp), r=["pvec"], w=["pvec"])
    S.add("vector", lambda e: e.tensor_scalar(out=pvec[64:128, PV_ALOG:PV_ALOG + 1], in0=pvec[64:128, PV_ALOG:PV_ALOG + 1],
                                              scalar1=-1.0, scalar2=None, op0=ALU.mult), r=["pvec"], w=["pvec"])

    def pv(col, n=1):
        return pvec[:, col:col + n]

    PS = [nc.alloc_psum_tensor("ps%d" % i, [128, 512], F32) for i in range(8)]
    state = {"evac": 0, "cast": 0, "gbank": 0, "mbank": 0}

    def evac_eng():
        state["evac"] ^= 1
        if state.get("force_evac"):
            return state["force_evac"]
        if EVAC_ENG is not None:
            return EVAC_ENG
        return "scalar" if state["evac"] else "vector"

    def copy_op(eng, out, in_):
        if eng == "scalar":
            return lambda e: e.activation(out=out, in_=in_, func=AF.Copy)
        return lambda e: e.tensor_copy(out=out, in_=in_)

    def mbank():
        state["mbank"] = (state["mbank"] + 1) % state.get("mb_n", 4)
        return state.get("mb_lo", 4) + state["mbank"]

    base_mark = A.mark()

    def gemm(Wt, item_base, K, col_sbs, tbs, act, act_keys, evac_cb, wst, wbf, mode="cast", Wbs=None):
        nseg = K // 2048
        assert nseg == 1 or len(tbs) == 1
        items = [(sb, seg) for sb in range(len(col_sbs)) for seg in range(nseg)]
        if mode == "bf16":
            bufs = list(wbf)
            for b in range(2):
                v = wst[b][:].rearrange("p a b -> p (a b)").bitcast(BF16)
                bufs.append(("alias", b, v[:, 0:4096].rearrange("p (a b) -> p a b", b=256)))
                bufs.append(("alias", b, v[:, 4096:8192].rearrange("p (a b) -> p a b", b=256)))
        else:
            bufs = list(wbf)
        NB = len(bufs)

        def buf_ap(j):
            bb = bufs[j]
            return bb[2] if isinstance(bb, tuple) else bb[:]

        def buf_keys(j):
            bb = bufs[j]
            return [("wbf", j)] + ([("wst", bb[1])] if isinstance(bb, tuple) else [])

        def load(ii):
            it = item_base + ii
            j = ii % NB
            if mode == "bf16":
                if isinstance(bufs[j], tuple):
                    S.add("sync", lambda e: e.dma_start(out=buf_ap(j), in_=Wbs[it].rearrange("p (a b) -> p a b", b=256)),
                          r=[("wbscr", id(Wbs), it)] + [("wbscr", id(Wbs), it, q) for q in range(4)], w=buf_keys(j), dma=True)
                else:
                    S.add("sync", lambda e: e.dma_start(out=bufs[j][:].rearrange("p a b -> p (a b)"), in_=Wbs[it]),
                          r=[("wbscr", id(Wbs), it)] + [("wbscr", id(Wbs), it, q) for q in range(4)], w=buf_keys(j), dma=True)
                return
            b = j
            S.add("sync", lambda e: e.dma_start(out=wst[b][:].rearrange("p a b -> p (a b)"), in_=Wt[it]),
                  w=[("wst", b)], dma=True)
            state["cast"] ^= 1
            ce = "gpsimd" if state["cast"] else "vector"
            if CAST_ENG is not None:
                ce = CAST_ENG
            if state.get("force_cast"):
                ce = state["force_cast"]
            S.add(ce, lambda e: e.tensor_copy(out=wbf[b][:], in_=wst[b][:]), r=[("wst", b)], w=[("wbf", b)])

        def store(ii):
            it = item_base + ii
            b = ii % NB
            S.add("sync", lambda e: e.dma_start(out=Wbs[it], in_=wbf[b][:].rearrange("p a b -> p (a b)")),
                  r=[("wbf", b)], w=[("wbscr", id(Wbs), it)], dma=True)

        for ii in range(min(NB - 1, len(items))):
            load(ii)
        banks = {}
        for ii, (sb, seg) in enumerate(items):
            if ii + NB - 1 < len(items):
                load(ii + NB - 1)
            col0, ncols = col_sbs[sb]
            j = ii % NB
            wap = buf_ap(j)
            for s0 in range(0, ncols, 128):
                n = min(128, ncols - s0)
                for (tbi, nt) in tbs:
                    if seg == 0:
                        state["gbank"] = (state["gbank"] + 1) % 4
                        banks[(sb, tbi, s0)] = state["gbank"]
                    bk = banks[(sb, tbi, s0)]
                    for kc in range(16):
                        a_ap = act(seg * 16 + kc, tbi)
                        st_ = (seg == 0 and kc == 0)
                        sp_ = (seg == nseg - 1 and kc == 15)
                        S.add("tensor",
                              lambda e, bk=bk, n=n, nt=nt, kc=kc, s0=s0, a_ap=a_ap, st_=st_, sp_=sp_, wap=wap:
                              e.matmul(PS[bk][0:n, 0:nt], lhsT=wap[:, kc, s0:s0 + n], rhs=a_ap, start=st_, stop=sp_),
                              r=[("wbf", j)] + act_keys(seg * 16 + kc, tbi), w=[("ps", bk)])
                    if seg == nseg - 1:
                        evac_cb(col0 + s0, n, tbi, nt, PS[bk][0:n, 0:nt], ("ps", bk))
            if mode == "cast_store":
                store(ii)

    def sbs_range(c0, c1):
        return [(c, min(256, c1 - c)) for c in range(c0, c1, 256)]

    XT = A.alloc([128, 16, TT], BF16)
    wst = [A.alloc([128, 16, 256], F32) for _ in range(2)]
    wbf = [A.alloc([128, 16, 256], BF16) for _ in range(2)]
    XST_AT = A.mark()
    xst = [A.alloc([128, 512], F32) for _ in range(2)]
    stA = [A.alloc([128, TT], F32) for _ in range(2)]
    stX = [A.alloc([128, TT + 6], F32) for _ in range(2)]
    cvo = [A.alloc([128, TT], F32) for _ in range(2)]
    cvb = [A.alloc([128, TT], BF16) for _ in range(2)]
    cph = A.alloc([128, 48, 3], F32)
    cst = nc.alloc_sbuf_tensor_at("cst_alias", [128, 48, 2, 3], F32, offset=XST_AT)
    S.add("sync", lambda e: e.dma_start(out=cph[:], in_=convp), w=["cph"], dma=True)
    for b_ in range(2):
        S.add("gpsimd", lambda e, b_=b_: e.memset(stX[b_][:, 0:3], 0.0), w=[("stXz", b_)])
    stB = [A.alloc([128, TT], BF16) for _ in range(2)]
    tks = A.alloc([128, 17, 128], F32)
    tkb = A.alloc([128, 17, 128], BF16)
    xTv = xT.rearrange("(c p) t -> p c t", p=128)
    stg = [stX[0][:, 3:3 + TT], stX[1][:, 3:3 + TT], cvo[0][:, 0:TT], cvo[1][:, 0:TT]]
    for kc in range(16):
        i_ = kc % 4
        S.add("sync", lambda e, kc=kc, i_=i_: e.dma_start(out=stg[i_], in_=xTv[:, kc, :]), w=[("stg", i_)], dma=True)
        eng = "vector" if kc % 2 == 0 else "scalar"
        S.add(eng, copy_op(eng, XT[:, kc, :], stg[i_]), r=[("stg", i_)], w=[("XT", kc, t0_) for (t0_, _n) in NTB])
    S.barrier()

    if stop == "A0":
        S.emit(nc)
        return nc
    stctr = {"n": 0}
    NT_P = TP // 128

    def tok_tiles():
        return [(i * 128, 128) for i in range(NT_P)] + [(TP, TS)]

    deferred = []

    def make_in_evac(kind, dst, row0, c_base):
        cur = {}

        def cb(col, n, tbi, nt, ps, pskey):
            t0 = NTB[tbi][0]
            if tbi == 0:
                stctr["n"] += 1
                cur["b"] = stctr["n"] % 2
            b = cur["b"]
            r0 = col - c_base + row0
            if kind == "xbc":
                eng = evac_eng()
                xo = t0 + 3 + (3 if t0 >= TP else 0)
                S.add(eng, copy_op(eng, stX[b][0:n, xo:xo + nt], ps), r=[pskey], w=[("stX", b, tbi)])
                if tbi == len(NTB) - 1:
                    while deferred:
                        deferred.pop(0)()
                    c = r0 // 128
                    sxk = [("stX", b, i) for i in range(len(NTB))] + [("stXz", b)]
                    S.add("scalar", lambda e: e.activation(out=stX[b][:, TP + 3:TP + 6], in_=cph[:, c, :], func=AF.Copy),
                          r=["cph"], w=[("stXh", b)])
                    sxk = sxk + [("stXh", b)]
                    S.add("scalar", lambda e: e.activation(out=cst[:, c, 0, :], in_=stX[b][:, TP:TP + 3], func=AF.Copy),
                          r=sxk, w=[("cst", c, 0), ("xst", 0)])
                    S.add("scalar", lambda e: e.activation(out=cst[:, c, 1, :], in_=stX[b][:, TT + 3:TT + 6], func=AF.Copy),
                          r=sxk, w=[("cst", c, 1)])
                    NQ = 4 if TP >= 1024 else 1
                    segs = [(q * (TP // NQ), q * (TP // NQ), TP // NQ) for q in range(NQ)] + [(TP, TP + 3, TS)]
                    for (o0, i0, L) in segs:
                        S.add("scalar", lambda e, o0=o0, i0=i0, L=L: e.activation(
                            out=cvo[b][:, o0:o0 + L], in_=stX[b][:, i0:i0 + L], func=AF.Identity,
                            scale=pvec[:, PV_CW + c:PV_CW + c + 1], bias=pvec[:, PV_CB + c:PV_CB + c + 1]),
                            r=sxk + ["pvec"], w=[("cvo", b, o0)])
                    for j in range(1, 4):
                        for (o0, i0, L) in segs:
                            S.add("vector", lambda e, o0=o0, i0=i0, L=L, j=j: e.scalar_tensor_tensor(
                                out=cvo[b][:, o0:o0 + L], in0=stX[b][:, i0 + j:i0 + j + L],
                                scalar=pvec[:, PV_CW + 48 * j + c:PV_CW + 48 * j + c + 1], in1=cvo[b][:, o0:o0 + L],
                                op0=ALU.mult, op1=ALU.add), r=sxk + ["pvec", ("cvo", b, o0)], w=[("cvo", b, o0)])
                    cvk = [("cvo", b, sg[0]) for sg in segs]

                    def tail(b=b, c=c, r0=r0, n=n, cvk=cvk):
                        S.add("scalar", lambda e: e.activation(out=cvo[b][:], in_=cvo[b][:], func=AF.Silu), r=cvk, w=cvk)
                        S.add("scalar", lambda e: e.activation(out=cvb[b][:], in_=cvo[b][:], func=AF.Copy), r=cvk,
                              w=[("cvb", b)])
                        if c < 32:
                            S.add(STQ, lambda e: e.dma_start(out=dst[r0:r0 + n, :], in_=cvo[b][:]), r=cvk,
                                  w=[("scr", kind, r0)], dma=True)
                        S.add(STQ, lambda e: e.dma_start(out=xbcTb[r0:r0 + n, :], in_=cvb[b][:]), r=[("cvb", b)],
                              w=[("scr", "xbcb", r0)], dma=True)
                    deferred.append(tail)
            elif kind in ("z", "dt"):
                eng = evac_eng()
                S.add(eng, copy_op(eng, stA[b][0:n, t0:t0 + nt], ps), r=[pskey], w=[("stA", b, tbi)])
            elif kind == "g":
                S.add("scalar", lambda e: e.activation(out=stA[b][0:n, t0:t0 + nt], in_=ps, func=AF.Sigmoid,
                                                       bias=pvec[0:n, PV_BG + r0 // 128:PV_BG + r0 // 128 + 1]),
                      r=[pskey, "pvec"], w=[("stA", b, tbi)])
            elif kind == "q":
                eng = evac_eng()
                S.add(eng, copy_op(eng, stB[b][0:n, t0:t0 + nt], ps), r=[pskey], w=[("stB", b, tbi)])
            elif kind == "k":
                S.add("scalar", copy_op("scalar", stA[b][0:n, t0:t0 + nt], ps), r=[pskey], w=[("stA", b, tbi)])
                S.add("gpsimd", copy_op("gpsimd", stB[b][0:n, t0:t0 + nt], stA[b][0:n, t0:t0 + nt]), r=[("stA", b, tbi)],
                      w=[("stB", b, tbi)])
            elif kind == "v":
                S.add("scalar", copy_op("scalar", stA[b][0:n, t0:t0 + nt], ps), r=[pskey], w=[("stA", b, tbi)])
            if tbi == len(NTB) - 1:
                allk = [("stA", b, i) for i in range(len(NTB))]
                allkb = [("stB", b, i) for i in range(len(NTB))]
                if kind in ("z", "dt", "g"):
                    S.add(STQ, lambda e: e.dma_start(out=dst[r0:r0 + n, :], in_=stA[b][0:n, :]), r=allk,
                          w=[("scr", kind, r0)], dma=True)
                if kind in ("q", "k"):
                    S.add(STQ, lambda e: e.dma_start(out=dst[r0:r0 + n, :], in_=stB[b][0:n, :]), r=allkb,
                          w=[("scr", kind, r0)], dma=True)
                if kind in ("k", "v"):
                    h = r0 // 128
                    tl = tok_tiles()
                    for g0 in range(0, len(tl), 4):
                        bk = mbank()
                        grp = tl[g0:g0 + 4]
                        for j, (tt0, tn) in enumerate(grp):
                            S.add("tensor", lambda e, bk=bk, j=j, tt0=tt0, tn=tn:
                                  e.matmul(PS[bk][0:tn, j * 128:(j + 1) * 128], lhsT=stA[b][:, tt0:tt0 + tn], rhs=ident, start=True, stop=True),
                                  r=allk + ["consts"], w=[("ps", bk)])
                        full = [x for x in grp if x[1] == 128]
                        eng = evac_eng()
                        if full:
                            nf = len(full)
                            S.add(eng, copy_op(eng, tks[:, g0:g0 + nf, :],
                                               PS[bk][:, 0:nf * 128].rearrange("p (a b) -> p a b", b=128)),
                                  r=[("ps", bk)], w=[("tks", g0)])
                        if len(full) < len(grp):
                            j = len(full)
                            S.add(eng, copy_op(eng, tks[0:TS, g0 + j, :], PS[bk][0:TS, j * 128:(j + 1) * 128]),
                                  r=[("ps", bk)], w=[("tks", g0, "s")])
                    tkk = [("tks", g0) for g0 in range(0, len(tl), 4)] + [("tks", (len(tl) - 1) // 4 * 4, "s")]
                    dsto = k_o if kind == "k" else v_o
                    S.add(STQ, lambda e: e.dma_start(
                        out=dsto[0:TP, h * 128:(h + 1) * 128].rearrange("(n p) f -> p n f", p=128),
                        in_=tks[:, 0:NT_P, :]), r=tkk, w=[("out", kind, h, 0)], dma=True)
                    S.add(STQ, lambda e: e.dma_start(out=dsto[TP:TT, h * 128:(h + 1) * 128], in_=tks[0:TS, NT_P, :]),
                          r=tkk, w=[("out", kind, h, 1)], dma=True)
                    if kind == "v":
                        S.add("gpsimd", lambda e: e.tensor_copy(out=tkb[:, 0:NT_P, :], in_=tks[:, 0:NT_P, :]), r=tkk,
                              w=["tkb"])
                        S.add("gpsimd", lambda e: e.tensor_copy(out=tkb[0:TS, NT_P, :], in_=tks[0:TS, NT_P, :]), r=tkk,
                              w=["tkb2"])
                        S.add(STQ, lambda e: e.dma_start(
                            out=vtm[0:TP, h * 128:(h + 1) * 128].rearrange("(n p) f -> p n f", p=128),
                            in_=tkb[:, 0:NT_P, :]), r=["tkb"], w=[("scr", "vtm", h, 0)], dma=True)
                        S.add(STQ, lambda e: e.dma_start(out=vtm[TP:TT, h * 128:(h + 1) * 128], in_=tkb[0:TS, NT_P, :]),
                              r=["tkb2"], w=[("scr", "vtm", h, 1)], dma=True)
        return cb

    tbs_all = [(i, nt) for i, (t0, nt) in enumerate(NTB)]

    def actA(kc, tbi):
        t0, nt = NTB[tbi]
        return XT[:, kc, t0:t0 + nt]

    def actA_keys(kc, tbi):
        return [("XT", kc, NTB[tbi][0])]

    plan = [("z", 0, 4096, zT), ("xbc", 4096, 10240, xbcT), ("dt", 10240, 10304, dtT), ("q", 10304, 12352, qT),
            ("k", 12352, 14400, kT), ("v", 14400, 16448, None), ("g", 16448, 20544, gT)]
    ibase = 0
    for kind, c0, c1, dst in plan:
        if stop is not None and stop.startswith("A1") and kind == stop[2:]:
            S.emit(nc)
            return nc
        if kind == "xbc":
            state["force_cast"] = "gpsimd"
            state["force_evac"] = "scalar"
        gemm(w_in, ibase, D, sbs_range(c0, c1), tbs_all, actA, actA_keys, make_in_evac(kind, dst, 0, c0), wst, wbf)
        state["force_cast"] = None
        state["force_evac"] = None
        while deferred:
            deferred.pop(0)()
        if kind == "xbc":
            for sq_ in range(2):
                for c4 in range(12):
                    bk = mbank()
                    for j in range(4):
                        c_ = c4 * 4 + j
                        S.add("tensor", lambda e, bk=bk, j=j, c_=c_, sq_=sq_: e.matmul(
                            PS[bk][0:3, j * 128:(j + 1) * 128], lhsT=cst[:, c_, sq_, :], rhs=ident, start=True, stop=True),
                            r=[("cst", c_, sq_), "consts"], w=[("ps", bk)])
                    S.add("vector", copy_op("vector", xst[1][0:3, :], PS[bk][0:3, :]), r=[("ps", bk)], w=[("xst", 1)])
                    S.add(STQ, lambda e, sq_=sq_, c4=c4: e.dma_start(out=conv_o[sq_][:, c4 * 512:(c4 + 1) * 512],
                                                                     in_=xst[1][0:3, :]),
                          r=[("xst", 1)], w=[("out", "conv", sq_, c4)], dma=True)
        ibase += len(sbs_range(c0, c1))

    if stop == "A":
        S.emit(nc)
        return nc
    S.barrier()
    A.reset(base_mark)

    co = A.alloc([128, 32, 128], F32)
    sb16 = A.alloc([128, 48, 128], BF16)
    xs_tm = A.alloc([128, 64, 64], BF16)
    B_tm = A.alloc([128, 8, 128], BF16)
    ddtm = A.alloc([128, 128], F32)
    rhsDs = [(A.alloc([128, 8, 128], BF16), A.alloc([128, 8, 128], BF16)) for _ in range(4)]
    expDs = [A.alloc([128, 8, 128], BF16) for _ in range(4)]
    ecsbs = [A.alloc([128, 8, 128], BF16) for _ in range(4)]
    state["mb_lo"], state["mb_n"] = 0, 8
    GM_AT = A.mark()
    Gm = A.alloc([128, 8, 128], BF16)
    dd = A.alloc([128, 128], F32, at=GM_AT)
    xd = A.alloc([128, 64, 64], BF16)
    xdd = A.alloc([128, 64, 64], BF16)
    yTs = [A.alloc([128, 32, 128], F32) for _ in range(2)]
    zbufs = [A.alloc([128, 32, 128], F32) for _ in range(2)]
    yT = yTs[0]
    Sst = A.alloc([128, 64, 64], F32)
    Sbf = A.alloc([128, 64, 64], BF16)
    ynbs = [zb[:].rearrange("p a b -> p (a b)").bitcast(BF16)[:, 0:4096].rearrange("p (a b) -> p a b", b=128) for zb in zbufs]
    small = A.alloc([128, 6, 64], F32)
    dAhl = small[:, 5, :].bitcast(BF16).rearrange("p (a b) -> p a b", b=64)
    rstds = [A.alloc([128, 8, 128], F32) for _ in range(2)]
    xbcTv = xbcT.rearrange("(c p) t -> p c t", p=128)
    xbcTbv = xbcTb.rearrange("(c p) t -> p c t", p=128)
    zTv = zT.rearrange("(c p) t -> p c t", p=128)
    ynTv = ynT.rearrange("(c p) t -> p c t", p=128)
    alt = {"n": 0}

    def vp():
        alt["n"] ^= 1
        return "vector" if alt["n"] else "gpsimd"

    pcf = [A.alloc([128, 1024], F32) for _ in range(3)]
    pcb = [A.alloc([128, 1024], BF16) for _ in range(2)]

    def precast_gen():
        jobs = []
        for nm, Wt_ in (("ssd", w_br_ssd), ("attn", w_br_attn), ("out", w_out), ("up", w_up), ("down", w_down)):
            for it in range(Wt_.shape[0]):
                for q in range(4):
                    jobs.append((nm, Wt_, it, q))

        def ld(k):
            nm, Wt_, it, q = jobs[k]
            b = k % 3
            S.add("scalar", lambda e: e.dma_start(out=pcf[b][:], in_=Wt_[it][:, q * 1024:(q + 1) * 1024]),
                  w=[("pcf", b)], dma=True)

        ld(0)
        ld(1)
        for k, (nm, Wt_, it, q) in enumerate(jobs):
            if k + 2 < len(jobs):
                ld(k + 2)
            b = k % 2
            fb = k % 3
            S.add("scalar", lambda e, b=b, fb=fb: e.activation(out=pcb[b][:], in_=pcf[fb][:], func=AF.Copy),
                  r=[("pcf", fb)], w=[("pcb", b)])
            Wbs = wb_scr[nm]
            S.add("scalar", lambda e, b=b, Wbs=Wbs, it=it, q=q: e.dma_start(out=Wbs[it][:, q * 1024:(q + 1) * 1024], in_=pcb[b][:]),
                  r=[("pcb", b)], w=[("wbscr", id(Wbs), it, q)], dma=True)
            yield

    precast = precast_gen() if PRECAST else iter(())
    tile_ctr = {"n": 0}
    gate_pending = []

    def pc_steps(n):
        for _ in range(n):
            try:
                next(precast)
            except StopIteration:
                return

    for si, (t0s, T, Lp) in enumerate(SEQS):
        tt = min(128, T)
        ntile = T // tt
        if si == 0:
            S.add("gpsimd", lambda e: e.memset(Sst[:], 0.0), w=["Sst"])
            S.add("gpsimd", lambda e: e.memset(Sbf[:], 0.0), w=["Sbf"])
        else:
            S.add("sync", lambda e: e.dma_start(out=Sst[:].rearrange("p a b -> p (a b)"), in_=s0T), w=["Sst"], dma=True)
            S.add("gpsimd", lambda e: e.tensor_copy(out=Sbf[:], in_=Sst[:]), r=["Sst"], w=["Sbf"])
        for ti in range(ntile):
            t0 = t0s + ti * tt
            cok = [("co", c) for c in range(32)]
            S.add("sync", lambda e, t0=t0, tt=tt: e.dma_start(out=sb16[:, :, 0:tt], in_=xbcTbv[:, :, t0:t0 + tt]),
                  r=[("scr", "xbcb", r) for r in range(0, CD, 128)], w=["sb16"], dma=True)
            S.add("sync", lambda e, t0=t0, tt=tt: e.dma_start(out=co[:, :, 0:tt], in_=xbcTv[:, 0:32, t0:t0 + tt]),
                  r=[("scr", "xbc", r) for r in range(0, DI, 128)], w=cok, dma=True)
            for g4 in range(10):
                bk = mbank()
                for j in range(4):
                    c = g4 * 4 + j
                    S.add("tensor", lambda e, bk=bk, j=j, c=c, tt=tt: e.matmul(
                        PS[bk][0:tt, j * 128:(j + 1) * 128], lhsT=sb16[:, c, 0:tt], rhs=identb[:], start=True, stop=True),
                        r=["sb16", "identb"], w=[("ps", bk)])
                eng = evac_eng()
                if g4 < 8:
                    dst_ap = xs_tm[0:tt, g4 * 8:(g4 + 1) * 8, :].rearrange("p a b -> p (a b)")
                    S.add(eng, copy_op(eng, dst_ap, PS[bk][0:tt, :]), r=[("ps", bk)], w=[("xs_tm", g4)])
                else:
                    S.add(eng, copy_op(eng, B_tm[0:tt, (g4 - 8) * 4:(g4 - 8) * 4 + 4, :].rearrange("p a b -> p (a b)"),
                                       PS[bk][0:tt, :]), r=[("ps", bk)], w=[("B_tm", g4 - 8)])
            xsk = [("xs_tm", g) for g in range(8)]
            S.add("sync", lambda e, t0=t0, tt=tt: e.dma_start(out=dd[0:64, 0:tt], in_=dtT[:, t0:t0 + tt]),
                  r=[("scr", "dt", 0)], w=["dd0", ("Gm", 0), ("Gm", 1)], dma=True)
            S.add("sync", lambda e, t0=t0, tt=tt: e.dma_start(out=dd[64:128, 0:tt], in_=dtT[:, t0:t0 + tt]),
                  r=[("scr", "dt", 0)], w=["dd1", ("Gm", 0), ("Gm", 1)], dma=True)
            S.add("scalar", lambda e, tt=tt: e.activation(out=dd[:, 0:tt], in_=dd[:, 0:tt], func=AF.Exp, bias=pv(PV_DTB)),
                  r=["dd0", "dd1", "pvec"], w=["dd"])
            S.add("scalar", lambda e, tt=tt: e.activation(out=dd[:, 0:tt], in_=dd[:, 0:tt], func=AF.Ln, bias=1.0),
                  r=["dd"], w=["dd"])
            S.add("vector", lambda e, tt=tt: e.tensor_scalar(out=dd[64:128, 0:tt], in0=dd[64:128, 0:tt],
                                                             scalar1=pvec[64:128, PV_ALOG:PV_ALOG + 1], scalar2=None,
                                                             op0=ALU.mult), r=["dd", "pvec"], w=["dd"])
            bk = mbank()
            S.add("tensor", lambda e, bk=bk, tt=tt: e.matmul(PS[bk][0:tt, 0:128], lhsT=dd[:, 0:tt], rhs=ident, start=True, stop=True),
                  r=["dd", "consts"], w=[("ps", bk)])
            S.add("vector", copy_op("vector", ddtm[0:tt, :], PS[bk][0:tt, 0:128]), r=[("ps", bk)], w=["ddtm"])
            S.add("vector", lambda e, tt=tt: e.tensor_copy(out=dAhl[0:tt, 0, :], in_=ddtm[0:tt, 64:128]), r=["ddtm"], w=["dAh"])
            S.add("vector", lambda e, tt=tt: e.tensor_tensor(out=dAhl[0:tt, 1, :], in0=ddtm[0:tt, 64:128], in1=dAhl[0:tt, 0, :],
                                                             op=ALU.subtract), r=["ddtm", "dAh"], w=["dAl"])
            bk = mbank()
            S.add("tensor", lambda e, bk=bk, tt=tt: e.matmul(PS[bk][0:tt, 0:64], lhsT=LE[0:tt, 0:tt], rhs=ddtm[0:tt, 64:128],
                                                             start=True, stop=True), r=["ddtm", "consts"], w=[("ps", bk)])
            S.add("tensor", lambda e, bk=bk, tt=tt: e.matmul(PS[bk][0:128, 64:128], lhsT=ones[0:tt, 0:128], rhs=ddtm[0:tt, 64:128],
                                                             start=True, stop=True), r=["ddtm", "consts"], w=[("ps", bk)])
            S.add("scalar", copy_op("scalar", small[:, 0, :], PS[bk][:, 64:128]), r=[("ps", bk)], w=["tot"])
            S.add("vector", lambda e, bk=bk, tt=tt: e.tensor_tensor(out=small[0:tt, 1, :], in0=small[0:tt, 0, :],
                                                                    in1=PS[bk][0:tt, 0:64], op=ALU.subtract),
                  r=["tot", ("ps", bk)], w=["tmpd"])
            S.add("scalar", lambda e, tt=tt: e.activation(out=small[0:tt, 2, :], in_=small[0:tt, 1, :], func=AF.Exp),
                  r=["tmpd"], w=["decay"])
            S.add("scalar", lambda e: e.activation(out=small[:, 3, :], in_=small[:, 0, :], func=AF.Exp), r=["tot"], w=["ecl"])
            S.add("vector", lambda e, tt=tt: e.tensor_tensor(out=small[0:tt, 4, :], in0=small[0:tt, 2, :], in1=ddtm[0:tt, 0:64],
                                                             op=ALU.mult), r=["decay", "ddtm"], w=["dtdec"])
            S.add("vector", lambda e, tt=tt: e.tensor_tensor(
                out=xd[0:tt], in0=xs_tm[0:tt], in1=ddtm[0:tt, 0:64].unsqueeze(2).broadcast_to([tt, 64, 64]), op=ALU.mult),
                r=xsk + ["ddtm"], w=["xd"])
            S.add("vector", lambda e, tt=tt: e.tensor_tensor(
                out=xdd[0:tt], in0=xs_tm[0:tt], in1=small[0:tt, 4, :].unsqueeze(2).broadcast_to([tt, 64, 64]), op=ALU.mult),
                r=xsk + ["dtdec"], w=["xdd"])
            for g4 in range(2):
                bk = mbank()
                for j in range(4):
                    g = g4 * 4 + j
                    S.add("tensor", lambda e, bk=bk, j=j, g=g, tt=tt: e.matmul(
                        PS[bk][0:tt, j * 128:j * 128 + tt], lhsT=sb16[:, 32 + g, 0:tt], rhs=sb16[:, 40 + g, 0:tt],
                        start=True, stop=True), r=["sb16"], w=[("ps", bk)])
                S.add("vector", lambda e, bk=bk, g4=g4, tt=tt: e.tensor_tensor(
                    out=Gm[0:tt, g4 * 4:g4 * 4 + 4, 0:tt],
                    in0=PS[bk][0:tt, :].rearrange("p (a b) -> p a b", b=128)[:, :, 0:tt],
                    in1=LE[0:tt, 0:tt].unsqueeze(1).broadcast_to([tt, 4, tt]), op=ALU.mult),
                    r=[("ps", bk), "consts"], w=[("Gm", g4), "dd", "dd0", "dd1"])
            par = tile_ctr["n"] % 2
            tile_ctr["n"] += 1

            def grp_task(g, bi, tt=tt, par=par):
                (rDh, rDl), eD, eC = rhsDs[bi], expDs[bi], ecsbs[bi]
                for x_, rD_ in ((0, rDh), (1, rDl)):
                    eng = vp()
                    S.add(eng, lambda e, x_=x_, rD_=rD_: e.tensor_tensor(
                        out=rD_[0:tt, :, 0:tt],
                        in0=dAhl[0:tt, x_, g * 8:g * 8 + 8].unsqueeze(2).broadcast_to([tt, 8, tt]),
                        in1=LEb[0:tt, 0:tt].unsqueeze(1).broadcast_to([tt, 8, tt]), op=ALU.mult),
                        r=["dAh", "dAl", "constsb"], w=[("rhsD", bi, x_)])
                yield
                hpm = min(8, 512 // tt)
                for m in range(8 // hpm):
                    h0 = m * hpm
                    bk = mbank()
                    for x_, rD_ in ((0, rDh), (1, rDl)):
                        S.add("tensor", lambda e, bk=bk, h0=h0, x_=x_, rD_=rD_: e.matmul(
                            PS[bk][0:tt, 0:hpm * tt].rearrange("p (a b) -> p a b", b=tt), lhsT=Ub[0:tt, 0:tt],
                            rhs=rD_[0:tt, h0:h0 + hpm, 0:tt], start=(x_ == 0), stop=(x_ == 1)),
                            r=[("rhsD", bi, 0), ("rhsD", bi, 1), "constsb"], w=[("ps", bk)])
                    S.add("scalar", lambda e, bk=bk, h0=h0: e.activation(
                        out=eD[0:tt, h0:h0 + hpm, 0:tt], in_=PS[bk][0:tt, 0:hpm * tt].rearrange("p (a b) -> p a b", b=tt),
                        func=AF.Exp), r=[("ps", bk)], w=[("expD", bi, m)])
                    yield
                    bk = mbank()
                    for x_, rD_ in ((0, rDh), (1, rDl)):
                        S.add("tensor", lambda e, bk=bk, h0=h0, x_=x_, rD_=rD_: e.matmul(
                            PS[bk][0:128, 0:hpm * tt].rearrange("p (a b) -> p a b", b=tt), lhsT=onesb[0:tt, 0:128],
                            rhs=rD_[0:tt, h0:h0 + hpm, 0:tt], start=(x_ == 0), stop=(x_ == 1)),
                            r=[("rhsD", bi, 0), ("rhsD", bi, 1), "constsb"], w=[("ps", bk)])
                    S.add("scalar", lambda e, bk=bk, h0=h0: e.activation(
                        out=eC[:, h0:h0 + hpm, 0:tt], in_=PS[bk][:, 0:hpm * tt].rearrange("p (a b) -> p a b", b=tt),
                        func=AF.Exp), r=[("ps", bk)], w=[("ecsb", bi, m)])
                    yield
                edk = [("expD", bi, m) for m in range(8 // hpm)]
                eck = [("ecsb", bi, m) for m in range(8 // hpm)]
                eng = vp()
                S.add(eng, lambda e: e.tensor_tensor(
                    out=eD[0:tt, :, 0:tt], in0=eD[0:tt, :, 0:tt], in1=Gm[0:tt, g:g + 1, 0:tt].broadcast_to([tt, 8, tt]),
                    op=ALU.mult), r=edk + [("Gm", g // 4)], w=edk)
                yield
                eng = vp()
                S.add(eng, lambda e: e.tensor_tensor(
                    out=eC[:, :, 0:tt], in0=eC[:, :, 0:tt], in1=sb16[:, 40 + g:41 + g, 0:tt].broadcast_to([128, 8, tt]),
                    op=ALU.mult), r=eck + ["sb16"], w=eck)
                yield
                bk = mbank()
                for pj in range(4):
                    for hx in range(2):
                        hl = pj * 2 + hx
                        h = g * 8 + hl
                        S.add("tensor", lambda e, bk=bk, pj=pj, hx=hx, hl=hl, h=h: e.matmul(
                            PS[bk][hx * 64:(hx + 1) * 64, pj * 128:pj * 128 + tt], lhsT=xd[0:tt, h, :],
                            rhs=eD[0:tt, hl, 0:tt], start=True, stop=False), r=["xd"] + edk, w=[("ps", bk)])
                        S.add("tensor", lambda e, bk=bk, pj=pj, hx=hx, hl=hl, h=h: e.matmul(
                            PS[bk][hx * 64:(hx + 1) * 64, pj * 128:pj * 128 + tt], lhsT=Sbf[:, h, :],
                            rhs=eC[:, hl, 0:tt], start=False, stop=True), r=["Sbf"] + eck, w=[("ps", bk)])
                    yield
                for pj in range(4):
                    c = g * 4 + pj
                    S.add("vector", lambda e, bk=bk, pj=pj, c=c: e.scalar_tensor_tensor(
                        out=yTs[par][:, c, 0:tt], in0=co[:, c, 0:tt], scalar=pvec[:, PV_DS + c:PV_DS + c + 1],
                        in1=PS[bk][:, pj * 128:pj * 128 + tt], op0=ALU.mult, op1=ALU.add),
                        r=[("co", c), "pvec", ("ps", bk)], w=[("yT", par, c)])
                yield

            gact = []
            if gate_pending:
                gact.append(("gate", gate_pending.pop(0)))
            for g in range(8):
                bi = g % 4
                while any(a[0] == bi for a in gact) or sum(1 for a in gact if a[0] != "gate") >= 3:
                    for a in list(gact):
                        try:
                            next(a[1])
                        except StopIteration:
                            gact.remove(a)
                gact.append((bi, grp_task(g, bi)))
                pc_steps(PC_PER)
            while gact:
                for a in list(gact):
                    try:
                        next(a[1])
                    except StopIteration:
                        gact.remove(a)
            for g in range(8):
                bk = mbank()
                S.add("tensor", lambda e, bk=bk, g=g, tt=tt: e.matmul(
                    PS[bk][:, 0:512], lhsT=B_tm[0:tt, g, :], rhs=xdd[0:tt, g * 8:(g + 1) * 8, :].rearrange("p a b -> p (a b)"), start=True, stop=True),
                    r=[("B_tm", 0), ("B_tm", 1), "xdd"], w=[("ps", bk)])
                S.add("gpsimd", lambda e, g=g: e.tensor_tensor(
                    out=Sst[:, g * 8:(g + 1) * 8, :], in0=Sst[:, g * 8:(g + 1) * 8, :],
                    in1=small[:, 3, g * 8:(g + 1) * 8].unsqueeze(2).broadcast_to([128, 8, 64]), op=ALU.mult),
                    r=["Sst", "ecl"], w=["Sst"])
                S.add("vector", lambda e, bk=bk, g=g: e.tensor_tensor(
                    out=Sst[:, g * 8:(g + 1) * 8, :], in0=Sst[:, g * 8:(g + 1) * 8, :],
                    in1=PS[bk][:, 0:512].rearrange("p (a b) -> p a b", b=64), op=ALU.add),
                    r=["Sst", ("ps", bk)], w=["Sst"])
            S.add("scalar", copy_op("scalar", Sbf[:], Sst[:]), r=["Sst"], w=["Sbf"])
            def gate_task(t0=t0, tt=tt, par=par):
                yTp, zb, rs, ynb = yTs[par], zbufs[par], rstds[par], ynbs[par]
                ytk = [("yT", par, c) for c in range(32)]
                zk_ = ("zbuf", par)
                S.add("sync", lambda e: e.dma_start(out=zb[:, :, 0:tt], in_=zTv[:, :, t0:t0 + tt]),
                      r=[("scr", "z", r) for r in range(0, DI, 128)], w=[zk_], dma=True)
                yield
                S.add("scalar", lambda e: e.activation(out=zb[:, :, 0:tt], in_=zb[:, :, 0:tt], func=AF.Silu), r=[zk_], w=[zk_])
                yield
                S.add("vector", lambda e: e.tensor_tensor(out=yTp[:, :, 0:tt], in0=yTp[:, :, 0:tt], in1=zb[:, :, 0:tt],
                                                          op=ALU.mult), r=ytk + [zk_], w=ytk)
                yield
                S.add("scalar", lambda e: e.activation(out=zb[:, :, 0:tt], in_=yTp[:, :, 0:tt], func=AF.Square), r=ytk, w=[zk_])
                yield
                for g4 in range(2):
                    bk = mbank()
                    for j in range(4):
                        g = g4 * 4 + j
                        for i4 in range(4):
                            S.add("tensor", lambda e, bk=bk, j=j, g=g, i4=i4: e.matmul(
                                PS[bk][:, j * 128:j * 128 + tt], lhsT=ones512, rhs=zb[:, g * 4 + i4, 0:tt],
                                start=(i4 == 0), stop=(i4 == 3)), r=[zk_, "consts"], w=[("ps", bk)])
                        yield
                    S.add("scalar", lambda e, bk=bk, g4=g4: e.activation(
                        out=rs[:, g4 * 4:g4 * 4 + 4, 0:tt], in_=PS[bk][:, :].rearrange("p (a b) -> p a b", b=128)[:, :, 0:tt],
                        func=AF.Ln, bias=RMS_EPS), r=[("ps", bk)], w=[("rstd", par, g4)])
                    yield
                    S.add("scalar", lambda e, g4=g4: e.activation(
                        out=rs[:, g4 * 4:g4 * 4 + 4, 0:tt], in_=rs[:, g4 * 4:g4 * 4 + 4, 0:tt], func=AF.Exp, scale=-0.5),
                        r=[("rstd", par, g4)], w=[("rstd", par, g4)])
                    yield
                for g in range(8):
                    S.add("vector", lambda e, g=g: e.tensor_tensor(
                        out=yTp[:, g * 4:g * 4 + 4, 0:tt], in0=yTp[:, g * 4:g * 4 + 4, 0:tt],
                        in1=rs[:, g:g + 1, 0:tt].broadcast_to([128, 4, tt]), op=ALU.mult),
                        r=ytk + [("rstd", par, g // 4)], w=[("yT", par, c) for c in range(g * 4, g * 4 + 4)])
                    yield
                S.add("vector", lambda e: e.tensor_tensor(
                    out=ynb[:, :, 0:tt], in0=yTp[:, :, 0:tt],
                    in1=pvec[:, PV_NW:PV_NW + 32].unsqueeze(2).broadcast_to([128, 32, tt]), op=ALU.mult),
                    r=ytk + ["pvec", zk_], w=[zk_])
                yield
                S.add("sync", lambda e: e.dma_start(out=ynTv[:, :, t0:t0 + tt], in_=ynb[:, :, 0:tt]),
                      r=[zk_], w=[("scr", "yn", t0)], dma=True)
                yield

            gate_pending.append(gate_task())
        while gate_pending:
            for _ in gate_pending.pop(0):
                pass
        for g4 in range(8):
            bk = mbank()
            for j in range(4):
                c = g4 * 4 + j
                S.add("tensor", lambda e, bk=bk, j=j, c=c: e.matmul(
                    PS[bk][:, j * 128:(j + 1) * 128], lhsT=Sst[:, 2 * c:2 * c + 2, :].rearrange("p a b -> p (a b)"), rhs=ident,
                    start=True, stop=True),
                    r=["Sst", "consts"], w=[("ps", bk)])
            eng = evac_eng()
            S.add(eng, copy_op(eng, yT[:, g4 * 4:g4 * 4 + 4, :], PS[bk][:, :].rearrange("p (a b) -> p a b", b=128)),
                  r=[("ps", bk)], w=[("yT", 0, c) for c in range(g4 * 4, g4 * 4 + 4)])
        S.add("sync", lambda e, si=si: e.dma_start(out=ssm_o[si].rearrange("(c p) n -> p c n", p=128), in_=yT[:]),
              r=[("yT", 0, c) for c in range(32)], w=[("out", "ssm", si)], dma=True)

    pc_steps(1 << 30)
    state["mb_lo"], state["mb_n"] = 4, 4
    if stop == "B":
        S.emit(nc)
        return nc
    S.barrier()
    A.reset(base_mark)

    NKM = max(TP, LP + TS)
    NSLOT = 5
    NVT = max(TP // 128, LP // 128 + 1)
    KThs = [A.alloc([128, NKM], BF16) for _ in range(2)]
    QThs = [A.alloc([128, max(TP, TS)], BF16) for _ in range(2)]
    Vhs = [A.alloc([128, NVT, 128], BF16) for _ in range(2)]
    ysts = [A.alloc([128, max(TP, TS)], BF16) for _ in range(2)]
    kst = A.alloc([128, LP], F32)
    vst = A.alloc([128, LP // 128, 128], F32)
    zss = [A.alloc([128, NKM], F32) for _ in range(NSLOT)]
    sps = [A.alloc([128, NKM], F32) for _ in range(NSLOT)]
    Pcs = [A.alloc([128, NKM + 1], F32) for _ in range(NSLOT)]
    Wbs = [A.alloc([128, NKM], BF16) for _ in range(NSLOT)]
    WTs = [A.alloc([128, NKM // 128 + 1, 128], BF16) for _ in range(NSLOT)]
    ntots = [A.alloc([128, 2], F32) for _ in range(NSLOT)]
    for sl in range(NSLOT):
        S.add("gpsimd", lambda e, sl=sl: e.memset(Pcs[sl][:, 0:1], 0.0), w=[("Pc0", sl)])

    def head_loads(si, h, hb):
        t0s, T, Lp = SEQS[si]
        KTh, QTh, Vh = KThs[hb], QThs[hb], Vhs[hb]
        rows = slice(h * 128, (h + 1) * 128)
        if Lp:
            S.add("sync", lambda e: e.dma_start(out=kst[:], in_=ckT[h]), w=["kst"], dma=True)
            S.add("vector", lambda e: e.tensor_copy(out=KTh[:, 0:LP], in_=kst[:]), r=["kst"], w=[("KTh_p", hb), ("KTh", hb)])
            S.add("sync", lambda e: e.dma_start(out=vst[:], in_=cv[:, rows].rearrange("(n p) f -> p n f", p=128)),
                  w=["vst"], dma=True)
            S.add("scalar", copy_op("scalar", Vh[:, 0:LP // 128, :], vst[:]), r=["vst"],
                  w=[("Vh_p", hb), ("Vh", hb)])
        S.add("sync", lambda e: e.dma_start(out=KTh[:, Lp:Lp + T], in_=kT[rows, t0s:t0s + T]),
              r=[("scr", "k", h * 128)], w=[("KTh", hb)], dma=True)
        S.add("sync", lambda e: e.dma_start(out=QTh[:, 0:T], in_=qT[rows, t0s:t0s + T]),
              r=[("scr", "q", h * 128)], w=[("QTh", hb)], dma=True)
        if T >= 128:
            S.add("sync", lambda e: e.dma_start(
                out=Vh[:, Lp // 128:Lp // 128 + T // 128, :],
                in_=vtm[t0s:t0s + T, rows].rearrange("(n p) f -> p n f", p=128)),
                r=[("scr", "vtm", h, 0)], w=[("Vh", hb)], dma=True)
        else:
            S.add("sync", lambda e: e.dma_start(out=Vh[0:T, Lp // 128, :], in_=vtm[t0s:t0s + T, rows]),
                  r=[("scr", "vtm", h, 1)], w=[("Vh", hb)], dma=True)

    def attn_task(si, h, hb, qi, sl, last):
        t0s, T, Lp = SEQS[si]
        tt = min(128, T)
        nq = T // tt
        KTh, QTh, Vh, yst = KThs[hb], QThs[hb], Vhs[hb], ysts[hb]
        zs, sp, Pc, Wb, WT, ntot = zss[sl], sps[sl], Pcs[sl], Wbs[sl], WTs[sl], ntots[sl]
        rows = slice(h * 128, (h + 1) * 128)
        q0 = qi * tt
        nk = Lp + (qi + 1) * tt
        for k0 in range(0, nk, 512):
            n = min(512, nk - k0)
            state["gbank"] = (state["gbank"] + 1) % 4
            bk = state["gbank"]
            S.add("tensor", lambda e, bk=bk, k0=k0, n=n: e.matmul(
                PS[bk][0:tt, 0:n], lhsT=QTh[:, q0:q0 + tt], rhs=KTh[:, k0:k0 + n], start=True, stop=True),
                r=[("QTh", hb), ("KTh", hb), ("KTh_p", hb)], w=[("ps", bk)])
            eng = evac_eng()
            if eng == "scalar":
                S.add(eng, lambda e, bk=bk, k0=k0, n=n: e.activation(
                    out=zs[0:tt, k0:k0 + n], in_=PS[bk][0:tt, 0:n], func=AF.Copy, scale=SB_SCALE),
                    r=[("ps", bk)], w=[("zs", sl, k0)])
            else:
                S.add(eng, lambda e, bk=bk, k0=k0, n=n: e.tensor_scalar(
                    out=zs[0:tt, k0:k0 + n], in0=PS[bk][0:tt, 0:n], scalar1=SB_SCALE, scalar2=None, op0=ALU.mult),
                    r=[("ps", bk)], w=[("zs", sl, k0)])
            yield
        zk = [("zs", sl, k0) for k0 in range(0, nk, 512)]
        spk = ("sp", sl)
        S.add("scalar", lambda e: e.activation(out=sp[0:tt, 0:nk], in_=zs[0:tt, 0:nk], func=AF.Exp), r=zk, w=[spk])
        yield
        S.add("scalar", lambda e: e.activation(out=sp[0:tt, 0:nk], in_=sp[0:tt, 0:nk], func=AF.Ln, bias=1.0), r=[spk], w=[spk])
        yield
        S.add("gpsimd", lambda e: e.tensor_tensor(out=sp[0:tt, nk - tt:nk], in0=sp[0:tt, nk - tt:nk], in1=Umat[0:tt, 0:tt],
                                                  op=ALU.mult), r=[spk, "consts"], w=[spk])
        yield
        S.add("vector", lambda e: e.tensor_tensor_scan(out=Pc[0:tt, 1:nk + 1], data0=sp[0:tt, 0:nk], data1=sp[0:tt, 0:nk],
                                                       initial=0.0, op0=ALU.add, op1=ALU.max),
              r=[spk, ("Pc0", sl)], w=[("Pc", sl)])
        yield
        S.add("vector", lambda e: e.tensor_scalar(out=ntot[0:tt, 0:1], in0=Pc[0:tt, nk:nk + 1], scalar1=-1.0, scalar2=None,
                                                  op0=ALU.mult), r=[("Pc", sl)], w=[("ntot", sl)])
        yield
        S.add("gpsimd", lambda e: e.tensor_tensor(out=zs[0:tt, 0:nk], in0=zs[0:tt, 0:nk], in1=Pc[0:tt, 0:nk], op=ALU.add),
              r=zk + [("Pc", sl), ("Pc0", sl)], w=zk)
        yield
        S.add("scalar", lambda e: e.activation(out=Wb[0:tt, 0:nk], in_=zs[0:tt, 0:nk], func=AF.Exp, bias=ntot[0:tt, 0:1]),
              r=zk + [("ntot", sl)], w=[("Wb", sl)])
        yield
        S.add("gpsimd", lambda e: e.tensor_tensor(out=Wb[0:tt, nk - tt:nk], in0=Wb[0:tt, nk - tt:nk], in1=Umat[0:tt, 0:tt],
                                                  op=ALU.mult), r=[("Wb", sl), "consts"], w=[("Wb", sl)])
        yield
        kbs = [(k0, min(128, nk - k0)) for k0 in range(0, nk, 128)]
        for g8 in range(0, len(kbs), 4):
            bk = mbank()
            grp = kbs[g8:g8 + 4]
            for j, (k0, kw) in enumerate(grp):
                S.add("tensor", lambda e, bk=bk, j=j, k0=k0, kw=kw: e.matmul(
                    PS[bk][0:kw, j * 128:j * 128 + tt], lhsT=Wb[0:tt, k0:k0 + kw], rhs=identb[0:tt, 0:tt],
                    start=True, stop=True), r=[("Wb", sl), "identb"], w=[("ps", bk)])
            eng = evac_eng()
            full = [x for x in grp if x[1] == 128]
            if full:
                nf = len(full)
                S.add(eng, copy_op(eng, WT[:, g8:g8 + nf, 0:tt],
                                   PS[bk][:, 0:nf * 128].rearrange("p (a b) -> p a b", b=128)[:, :, 0:tt]),
                      r=[("ps", bk)], w=[("WT", sl, g8)])
            if len(full) < len(grp):
                j = len(full)
                kw = grp[j][1]
                S.add(eng, copy_op(eng, WT[0:kw, g8 + j, 0:tt], PS[bk][0:kw, j * 128:j * 128 + tt]),
                      r=[("ps", bk)], w=[("WT", sl, g8, "s")])
            yield
        wtk = [("WT", sl, g8) for g8 in range(0, len(kbs), 4)] + [("WT", sl, g8, "s") for g8 in range(0, len(kbs), 4)]
        bk = mbank()
        for j, (k0, kw) in enumerate(kbs):
            S.add("tensor", lambda e, bk=bk, j=j, kw=kw, nkb=len(kbs): e.matmul(
                PS[bk][:, 0:tt], lhsT=Vh[0:kw, j, :], rhs=WT[0:kw, j, 0:tt], start=(j == 0), stop=(j == nkb - 1)),
                r=wtk + [("Vh", hb), ("Vh_p", hb)], w=[("ps", bk)])
        eng = evac_eng()
        S.add(eng, copy_op(eng, yst[:, q0:q0 + tt], PS[bk][:, 0:tt]), r=[("ps", bk)], w=[("yst", hb, qi)])
        yield
        if last:
            S.add("sync", lambda e: e.dma_start(out=ysbT[rows, t0s:t0s + T], in_=yst[:, 0:T]),
                  r=[("yst", hb, q) for q in range(nq)], w=[("scr", "ysb", h, si)], dma=True)
            yield

    def attn_tasks():
        hc = 0
        tc = 0
        for si, (t0s, T, Lp) in enumerate(SEQS):
            tt = min(128, T)
            nq = T // tt
            for h in range(AH):
                hb = hc % 2
                hc += 1
                for qi in range(nq):
                    yield (si, h, hb, qi, tc % NSLOT, qi == nq - 1, qi == 0)
                    tc += 1

    active = []
    for (si, h, hb, qi, sl, last, first) in attn_tasks():
        if first:
            while any(a[2] == hb for a in active):
                for a in list(active):
                    try:
                        next(a[1])
                    except StopIteration:
                        active.remove(a)
            head_loads(si, h, hb)
        while any(a[0] == sl for a in active) or len(active) >= NSLOT:
            for a in list(active):
                try:
                    next(a[1])
                except StopIteration:
                    active.remove(a)
        active.append((sl, attn_task(si, h, hb, qi, sl, last), hb))
    while active:
        for a in list(active):
            try:
                next(a[1])
            except StopIteration:
                active.remove(a)

    if stop == "C":
        S.emit(nc)
        return nc
    S.barrier()
    A.reset(base_mark)

    wst = [A.alloc([128, 16, 256], F32) for _ in range(2)]
    wbf = [A.alloc([128, 16, 256], BF16) for _ in range(2)]
    P1 = A.mark()
    ynB = A.alloc([128, 32, 512], BF16)
    ysB = A.alloc([128, 16, 512], BF16)
    MGb = A.alloc([128, 16, 512], BF16)
    Hb = A.alloc([128, 64, 512], BF16, at=P1)
    MG = A.alloc([128, 16, 512], F32)
    X1b = A.alloc([128, 16, 512], BF16, at=A.off - 16 * 512 * 4)
    R1 = A.alloc([128, 16, 512], F32)
    gst = [A.alloc([128, 512], F32) for _ in range(2)]
    tmpd = A.alloc([128, 512], F32)
    tmpb = A.alloc([128, 512], F32)
    stat = A.alloc([128, 2, 512], F32)
    yout = A.alloc([128, 4, 512], F32)
    ynTv = ynT.rearrange("(c p) t -> p c t", p=128)
    ysbTv = ysbT.rearrange("(c p) t -> p c t", p=128)
    gctr = {"n": 0}

    def layer_norm(nt, gcol, bcol, make_bf):
        r1k = [("R1", c) for c in range(16)]
        bk = mbank()
        for c in range(16):
            S.add("tensor", lambda e, bk=bk, c=c: e.matmul(PS[bk][:, 0:nt], lhsT=ones, rhs=R1[:, c, 0:nt],
                                                           start=(c == 0), stop=(c == 15)), r=r1k + ["consts"], w=[("ps", bk)])
        S.add("scalar", lambda e, bk=bk: e.activation(out=stat[:, 0, 0:nt], in_=PS[bk][:, 0:nt], func=AF.Copy,
                                                      scale=1.0 / D), r=[("ps", bk)], w=["mean"])
        for c in range(16):
            eng = vp()
            S.add(eng, lambda e, c=c: e.tensor_tensor(out=R1[:, c, 0:nt], in0=R1[:, c, 0:nt], in1=stat[:, 0, 0:nt],
                                                      op=ALU.subtract), r=[("R1", c), "mean"], w=[("R1", c)])
        bk = mbank()
        for c in range(16):
            sqb = gst[c % 2]
            S.add("scalar", lambda e, c=c, sqb=sqb: e.activation(out=sqb[:, 0:nt], in_=R1[:, c, 0:nt], func=AF.Square),
                  r=[("R1", c)], w=[("gst", c % 2)])
            S.add("tensor", lambda e, bk=bk, c=c, sqb=sqb: e.matmul(PS[bk][:, 0:nt], lhsT=ones, rhs=sqb[:, 0:nt],
                                                                    start=(c == 0), stop=(c == 15)),
                  r=[("gst", c % 2), "consts"], w=[("ps", bk)])
        S.add("scalar", lambda e, bk=bk: e.activation(out=stat[:, 1, 0:nt], in_=PS[bk][:, 0:nt], func=AF.Sqrt,
                                                      scale=1.0 / D, bias=LN_EPS), r=[("ps", bk)], w=["rstd"])
        S.add("vector", lambda e: e.reciprocal(out=stat[:, 1, 0:nt], in_=stat[:, 1, 0:nt]), r=["rstd"], w=["rstd"])
        for c in range(16):
            eng = vp()
            S.add(eng, lambda e, c=c: e.tensor_tensor(out=R1[:, c, 0:nt], in0=R1[:, c, 0:nt], in1=stat[:, 1, 0:nt],
                                                      op=ALU.mult), r=[("R1", c), "rstd"], w=[("R1", c)])
            S.add("vector", lambda e, c=c: e.tensor_scalar(out=R1[:, c, 0:nt], in0=R1[:, c, 0:nt],
                                                           scalar1=pvec[:, gcol + c:gcol + c + 1],
                                                           scalar2=pvec[:, bcol + c:bcol + c + 1], op0=ALU.mult, op1=ALU.add),
                  r=[("R1", c), "pvec"], w=[("R1", c)])
            if make_bf:
                S.add("gpsimd", lambda e, c=c: e.tensor_copy(out=X1b[:, c, 0:nt], in_=R1[:, c, 0:nt]), r=[("R1", c)],
                      w=[("X1b", c)] + [("MG", cc) for cc in range(8)])

    gTv = gT.rearrange("(c p) t -> p c t", p=128)
    hbk = [("Hb", c) for c in range(64)]
    mgk = [("MG", c) for c in range(16)]
    x1k = [("X1b", c) for c in range(16)]

    def phase_d_loads(tbi, t0, nt):
        S.add("sync", lambda e: e.dma_start(out=ynB[:, :, 0:nt], in_=ynTv[:, :, t0:t0 + nt]),
              r=[("scr", "yn", t) for t in range(t0, t0 + nt, 128 if nt >= 128 else nt)], w=["ynB"] + hbk, dma=True)
        S.add("sync", lambda e: e.dma_start(out=ysB[:, :, 0:nt], in_=ysbTv[:, :, t0:t0 + nt]),
              r=[("scr", "ysb", h, 0 if t0 < TP else 1) for h in range(AH)], w=["ysB"] + hbk, dma=True)

    def phase_d_block(tbi, t0, nt):
        wmode = "bf16" if (PRECAST or tbi > 0) else "cast_store"
        if tbi == 0:
            phase_d_loads(tbi, t0, nt)

        def gload(c, half, src=None):
            gctr["n"] += 1
            b = gctr["n"] % 2
            if src is None:
                S.add("sync", lambda e: e.dma_start(out=gst[b][:, 0:nt], in_=gTv[:, half * 16 + c, t0:t0 + nt]),
                      r=[("scr", "g", (half * 16 + c) * 128)], w=[("gst", b)], dma=True)
            else:
                S.add("sync", lambda e: e.dma_start(out=gst[b][:, 0:nt], in_=src), w=[("gst", b)], dma=True)
            return b

        def ev1(col, n, tbi_, nt_, ps, pskey):
            c = col // 128
            b = gload(c, 0)
            S.add("vector", lambda e: e.tensor_tensor(out=MG[:, c, 0:nt], in0=ps, in1=gst[b][:, 0:nt], op=ALU.mult),
                  r=[pskey, ("gst", b)], w=[("MG", c)] + (x1k if c < 8 else []))
        gemm(w_br_ssd, 0, DI, sbs_range(0, D), [(tbi, nt)], lambda kc, t: ynB[:, kc, 0:nt], lambda kc, t: ["ynB"], ev1, wst, wbf, wmode, wb_scr["ssd"])

        def ev2(col, n, tbi_, nt_, ps, pskey):
            c = col // 128
            b = gload(c, 1)
            S.add("vector", lambda e: e.tensor_tensor(out=tmpd[:, 0:nt], in0=ps, in1=gst[b][:, 0:nt], op=ALU.mult),
                  r=[pskey, ("gst", b)], w=["tmpd"])
            S.add("gpsimd", lambda e: e.tensor_tensor(out=MGb[:, c, 0:nt], in0=MG[:, c, 0:nt], in1=tmpd[:, 0:nt], op=ALU.add),
                  r=[("MG", c), "tmpd"], w=[("MGb", c)] + hbk)
        gemm(w_br_attn, 0, D, sbs_range(0, D), [(tbi, nt)], lambda kc, t: ysB[:, kc, 0:nt], lambda kc, t: ["ysB"], ev2, wst, wbf, wmode, wb_scr["attn"])

        def ev3(col, n, tbi_, nt_, ps, pskey):
            c = col // 128
            b = gload(c, 0, src=xTv[:, c, t0:t0 + nt])
            S.add("vector", lambda e: e.scalar_tensor_tensor(out=R1[:, c, 0:nt], in0=gst[b][:, 0:nt], scalar=ALPHA, in1=ps,
                                                             op0=ALU.mult, op1=ALU.add),
                  r=[pskey, ("gst", b)], w=[("R1", c)])
        gemm(w_out, 0, D, sbs_range(0, D), [(tbi, nt)], lambda kc, t: MGb[:, kc, 0:nt], lambda kc, t: [("MGb", kc)], ev3, wst, wbf, wmode, wb_scr["out"])
        layer_norm(nt, PV_L1G, PV_L1B, True)

        def ev4(col, n, tbi_, nt_, ps, pskey):
            c = col // 128
            S.add("scalar", lambda e: e.activation(out=tmpb[:, 0:nt], in_=ps, func=AF.Relu, bias=pvec[:, PV_BU + c:PV_BU + c + 1]),
                  r=[pskey, "pvec"], w=["tmpb"])
            S.add("vector", lambda e: e.tensor_tensor(out=Hb[:, c, 0:nt], in0=tmpb[:, 0:nt], in1=tmpb[:, 0:nt], op=ALU.mult),
                  r=["tmpb"], w=[("Hb", c), "ynB", "ysB"] + [("MGb", cc) for cc in range(16)])
        gemm(w_up, 0, D, sbs_range(0, DFF), [(tbi, nt)], lambda kc, t: X1b[:, kc, 0:nt], lambda kc, t: [("X1b", kc)], ev4, wst, wbf, wmode, wb_scr["up"])

        def ev5(col, n, tbi_, nt_, ps, pskey):
            c = col // 128
            S.add("gpsimd", lambda e: e.tensor_scalar(out=R1[:, c, 0:nt], in0=R1[:, c, 0:nt], scalar1=ALPHA,
                                                      scalar2=pvec[:, PV_BD + c:PV_BD + c + 1], op0=ALU.mult, op1=ALU.add),
                  r=[("R1", c), "pvec"], w=[("R1", c)])
            S.add("vector", lambda e: e.tensor_tensor(out=R1[:, c, 0:nt], in0=R1[:, c, 0:nt], in1=ps, op=ALU.add),
                  r=[("R1", c), pskey], w=[("R1", c)])
        gemm(w_down, 0, DFF, sbs_range(0, D), [(tbi, nt)], lambda kc, t: Hb[:, kc, 0:nt], lambda kc, t: [("Hb", kc)], ev5, wst, wbf, wmode, wb_scr["down"])
        if tbi + 1 < len(NTB):
            phase_d_loads(tbi + 1, NTB[tbi + 1][0], NTB[tbi + 1][1])
        layer_norm(nt, PV_L2G, PV_L2B, False)

        ttl = [(i, min(128, nt - i)) for i in range(0, nt, 128)]
        for ti_, (o, tn) in enumerate(ttl):
            for c4 in range(4):
                bk = mbank()
                for j in range(4):
                    c = c4 * 4 + j
                    S.add("tensor", lambda e, bk=bk, j=j, c=c, o=o, tn=tn: e.matmul(
                        PS[bk][0:tn, j * 128:(j + 1) * 128], lhsT=R1[:, c, o:o + tn], rhs=ident, start=True, stop=True),
                        r=[("R1", c), "consts"], w=[("ps", bk)])
                eng = evac_eng()
                S.add(eng, copy_op(eng, yout[0:tn, c4, :], PS[bk][0:tn, :]), r=[("ps", bk)], w=[("yout", c4)])
            S.add("sync", lambda e, o=o, tn=tn, t0=t0: e.dma_start(
                out=y_o[t0 + o:t0 + o + tn, :], in_=yout[0:tn].rearrange("p a b -> p (a b)")),
                r=[("yout", c4) for c4 in range(4)], w=[("out", "y", t0 + o)], dma=True)

    for tbi, (t0, nt) in enumerate(NTB):
        phase_d_block(tbi, t0, nt)

    S.emit(nc)
    return nc


def host_inputs(b, TP, inp, tw):
    f = np.float32
    xp = inp["x_prompt"][b][:TP]
    xs = inp["x_sample"][b]
    xT = np.ascontiguousarray(np.concatenate([xp, xs], axis=0).T).astype(f)
    convp = np.ascontiguousarray(inp["cache_conv"][0, b].T.reshape(48, 128, 3).transpose(1, 0, 2)).astype(f)
    s0T = np.ascontiguousarray(inp["state_ssm"][0, b].reshape(NH * HP, NS).T).astype(f)
    ckT = np.ascontiguousarray(inp["cache_k"][0, b].transpose(1, 2, 0)).astype(f)
    cvv = np.ascontiguousarray(inp["cache_v"][0, b].reshape(LP, D)).astype(f)
    pvec = np.zeros((128, NPV), f)

    def col(v):
        return np.asarray(v, f).reshape(-1, 128).T

    pvec[:, PV_BG:PV_BG + 32] = col(inp["b_gate"][0])
    for j in range(4):
        pvec[:, PV_CW + 48 * j:PV_CW + 48 * (j + 1)] = col(inp["conv_w"][0, j])
    pvec[:, PV_CB:PV_CB + 48] = col(inp["conv_b"][0])
    pvec[:, PV_NW:PV_NW + 32] = col(inp["ssm_norm_w"][0])
    pvec[:, PV_L1G:PV_L1G + 16] = col(inp["ln1_g"][0])
    pvec[:, PV_L1B:PV_L1B + 16] = col(inp["ln1_b"][0])
    pvec[:, PV_L2G:PV_L2G + 16] = col(inp["ln2_g"][0])
    pvec[:, PV_L2B:PV_L2B + 16] = col(inp["ln2_b"][0])
    pvec[:, PV_BD:PV_BD + 16] = col(inp["b_down"][0])
    pvec[:, PV_BU:PV_BU + 64] = col(inp["b_up"][0])
    pvec[:, PV_DS:PV_DS + 32] = col(np.repeat(np.asarray(inp["d_skip"][0], f), HP))
    pvec[0:64, PV_DTB] = inp["dt_bias"][0]
    pvec[64:128, PV_DTB] = inp["dt_bias"][0]
    pvec[64:128, PV_ALOG] = inp["a_log"][0]
    r = np.arange(128)
    consts = np.concatenate([np.eye(128, dtype=f), np.ones((128, 128), f), (r[:, None] > r[None, :]).astype(f),
                             (r[:, None] <= r[None, :]).astype(f), np.full((128, 128), 1.0 / 512, f)], axis=1)
    m = {"xT": xT, "convp": convp, "s0T": s0T, "ckT": ckT, "cv": cvv, "pvec": pvec, "consts": consts}
    m.update(tw)
    return m


def _tile_w(W, ranges):
    K = W.shape[0]
    tiles = []
    for (c0, c1) in ranges:
        for col0 in range(c0, c1, 256):
            ncols = min(256, c1 - col0)
            for seg in range(K // 2048):
                blk = W[seg * 2048:(seg + 1) * 2048, col0:col0 + ncols].reshape(16, 128, ncols).transpose(1, 0, 2)
                t = np.zeros((128, 16, 256), np.float32)
                t[:, :, :ncols] = blk
                tiles.append(t.reshape(128, 4096))
    return np.ascontiguousarray(np.stack(tiles))


def tiled_weights(inp):
    out = {"w_in": _tile_w(np.asarray(inp["w_in"][0], np.float32), WIN_PLAN)}
    for wn in ("w_br_ssd", "w_br_attn", "w_out", "w_up", "w_down"):
        W = np.asarray(inp[wn][0], np.float32)
        out[wn] = _tile_w(W, [(0, W.shape[1])])
    return out


_NC_CACHE = {}


def run(inp, TP, cores, dbg=False):
    key = (TP, dbg)
    if key not in _NC_CACHE:
        _NC_CACHE[key] = build(TP, dbg)
    nc = _NC_CACHE[key]
    tw = tiled_weights(inp)
    in_maps = [host_inputs(b, TP, inp, tw) for b in cores]
    res = run_bass_kernel_spmd(nc, in_maps, core_ids=list(range(len(cores))))
    return res.results


def kernel(**inputs):
    inp = {k: np.asarray(v) for k, v in inputs.items()}
    TP = inp["x_prompt"].shape[1]
    B = inp["x_prompt"].shape[0]
    res = run(inp, TP, list(range(B)))
    f = np.float32
    y_p = np.stack([r["y_o"][:TP] for r in res]).astype(f)
    y_s = np.stack([r["y_o"][TP:] for r in res]).astype(f)
    conv_p = np.stack([r["conv_p"] for r in res])[None].astype(f)
    conv_s = np.stack([r["conv_s"] for r in res])[None].astype(f)
    ssm_p = np.stack([r["ssm_p"].reshape(NH, HP, NS) for r in res])[None].astype(f)
    ssm_s = np.stack([r["ssm_s"].reshape(NH, HP, NS) for r in res])[None].astype(f)
    k_p = np.stack([r["k_o"][:TP].reshape(TP, AH, AD) for r in res])[None].astype(f)
    k_s = np.stack([r["k_o"][TP:].reshape(TS, AH, AD) for r in res])[None].astype(f)
    v_p = np.stack([r["v_o"][:TP].reshape(TP, AH, AD) for r in res])[None].astype(f)
    v_s = np.stack([r["v_o"][TP:].reshape(TS, AH, AD) for r in res])[None].astype(f)
    return (y_p, y_s, conv_p, ssm_p, k_p, v_p, conv_s, ssm_s, k_s, v_s)
```
